# Optimizing a Trainium2 kernel written in Bass

```python
import jax, jax.numpy as jnp
from jax import lax
import numpy as np

D_MODEL = 1024
BATCH = 16
SEQ = 2048
DEPTH = 1
DEC_BATCH = 32
DEC_SEQ = 64
PAST_LEN = 4096

CHUNK = 64
GDN_HEADS = 8
GDN_DK = 128
GDN_DV = 128
GDN_QK_DIM = GDN_HEADS * GDN_DK
GDN_V_DIM = GDN_HEADS * GDN_DV
GDN_CONV_DIM = 2 * GDN_QK_DIM + GDN_V_DIM
CONV_W = 4
SWA_HQ = 16
SWA_HKV = 4
SWA_HD = 64
SWA_GROUP = SWA_HQ // SWA_HKV
WINDOW = 128
WIN_CHUNKS = WINDOW // CHUNK
D_FF = 2816
N_MOD = 9
EPS = 1e-6
IN_SIZES = (GDN_CONV_DIM, GDN_V_DIM, GDN_HEADS, GDN_HEADS, SWA_HQ * SWA_HD, SWA_HKV * SWA_HD, SWA_HKV * SWA_HD, 2 * D_MODEL)
IN_DIM = GDN_CONV_DIM + GDN_V_DIM + 2 * GDN_HEADS + SWA_HQ * SWA_HD + 2 * SWA_HKV * SWA_HD + 2 * D_MODEL

kernel_name = 'streaming_gdn_swa_macaron_adaln'


def rms_norm(x, gain):
    xf = x.astype(jnp.float32)
    y = xf * lax.rsqrt(jnp.mean(xf * xf, axis=-1, keepdims=True) + EPS)
    return (y * gain.astype(jnp.float32)).astype(x.dtype)


def l2_norm(x):
    xf = x.astype(jnp.float32)
    return xf * lax.rsqrt(jnp.sum(xf * xf, axis=-1, keepdims=True) + EPS)


def modulate(x, gain, shift, scale):
    return rms_norm(x, gain) * (1 + scale) + shift


def swiglu(h, w_in, w_out):
    gate, up = jnp.split(h @ w_in, 2, axis=-1)
    return (jax.nn.silu(gate) * up) @ w_out


def causal_conv(x, prefix, w):
    T = x.shape[1]
    xp = jnp.concatenate([prefix.astype(x.dtype), x], axis=1)
    y = xp[:, 0:T] * w[0]
    for i in range(1, CONV_W):
        y = y + xp[:, i:i + T] * w[i]
    return jax.nn.silu(y), xp[:, -(CONV_W - 1):]


def gated_delta_chunked(q, k, v, g, beta, S0):
    B, T, H, DK = q.shape
    DV = v.shape[-1]
    L = min(CHUNK, T)
    NC = T // L

    def blk(a):
        a = a.reshape((B, NC, L, H) + a.shape[3:])
        return jnp.moveaxis(a, 3, 1)

    q, k, v, g, beta = blk(q), blk(k), blk(v), blk(g), blk(beta)
    G = jnp.cumsum(g, axis=-1)
    incl = jnp.tril(jnp.ones((L, L), bool))
    strict = jnp.tril(jnp.ones((L, L), bool), -1)
    gamma = jnp.exp(jnp.where(incl, G[..., :, None] - G[..., None, :], -jnp.inf))
    kk = jnp.einsum('bhcid,bhcjd->bhcij', k, k)
    A = jnp.eye(L, dtype=jnp.float32) + jnp.where(strict, beta[..., :, None] * kk * gamma, 0.0)
    rhs = jnp.concatenate([v * beta[..., None], k * (beta * jnp.exp(G))[..., None]], axis=-1)
    X = lax.linalg.triangular_solve(A, rhs, left_side=True, lower=True, unit_diagonal=True)
    u, w = X[..., :DV], X[..., DV:]
    qk = jnp.einsum('bhcid,bhcjd->bhcij', q, k) * gamma
    q_dec = q * jnp.exp(G)[..., None]
    k_dec = k * jnp.exp(G[..., -1:] - G)[..., None]
    d_last = jnp.exp(G[..., -1])

    def step(S, xs):
        u_c, w_c, qk_c, qd_c, kd_c, dl_c = xs
        v_new = u_c - jnp.einsum('bhid,bhde->bhie', w_c, S)
        o = jnp.einsum('bhid,bhde->bhie', qd_c, S) + jnp.einsum('bhij,bhje->bhie', qk_c, v_new)
        S = dl_c[..., None, None] * S + jnp.einsum('bhid,bhie->bhde', kd_c, v_new)
        return S, o

    xs = tuple(jnp.moveaxis(a, 2, 0) for a in (u, w, qk, q_dec, k_dec, d_last))
    S, o = lax.scan(step, S0, xs)
    o = jnp.moveaxis(o, 0, 2).reshape(B, H, T, DV)
    return jnp.transpose(o, (0, 2, 1, 3)), S


def sink_attention(q, k, v, mask, sinks):
    s = jnp.einsum('bclkgd,bcskd->bckgls', q, k, preferred_element_type=jnp.float32) * (SWA_HD ** -0.5)
    s = jnp.where(mask[None, :, None, None], s, -jnp.inf)
    sink = sinks.astype(jnp.float32).reshape(SWA_HKV, SWA_GROUP)[None, None, :, :, None, None]
    m = jnp.maximum(jnp.max(s, axis=-1, keepdims=True), sink)
    p = jnp.exp(s - m)
    p = p / (jnp.sum(p, axis=-1, keepdims=True) + jnp.exp(sink - m))
    return jnp.einsum('bckgls,bcskd->bclkgd', p.astype(v.dtype), v)


def swa_prompt(q, k, v, sinks):
    B, T = q.shape[:2]
    NC = T // CHUNK
    qb = q.reshape(B, NC, CHUNK, SWA_HKV, SWA_GROUP, SWA_HD)

    def band(a):
        pad = jnp.zeros((B, WINDOW) + a.shape[2:], a.dtype)
        ac = jnp.concatenate([pad, a], axis=1).reshape((B, NC + WIN_CHUNKS, CHUNK) + a.shape[2:])
        return jnp.concatenate([ac[:, j:j + NC] for j in range(WIN_CHUNKS + 1)], axis=2)

    key_chunk = jnp.arange(NC)[:, None] - WIN_CHUNKS + jnp.arange(WIN_CHUNKS + 1)[None, :]
    valid = jnp.repeat(key_chunk >= 0, CHUNK, axis=1)
    o = sink_attention(qb, band(k), band(v), valid[:, None, :], sinks)
    return o.reshape(B, T, SWA_HQ * SWA_HD)


def swa_sample(q, k, v, k_cache, v_cache, sinks):
    B, T = q.shape[:2]
    kf = jnp.concatenate([k_cache.astype(k.dtype), k], axis=1)
    vf = jnp.concatenate([v_cache.astype(v.dtype), v], axis=1)
    qb = q.reshape(B, 1, T, SWA_HKV, SWA_GROUP, SWA_HD)
    mask = jnp.ones((1, 1, WINDOW + T), bool)
    o = sink_attention(qb, kf[:, None], vf[:, None], mask, sinks)
    return o.reshape(B, T, SWA_HQ * SWA_HD), kf[:, -WINDOW:], vf[:, -WINDOW:]


def trunk_layer(x, c, conv_prefix, S0, k_cache, v_cache, lp, is_prompt):
    (w_ada, b_ada, norm_ffn1, ffn1_w_in, ffn1_w_out, norm_mix, w_in, gdn_conv_w,
     gdn_a_log, gdn_dt_bias, gdn_norm, swa_q_norm, swa_k_norm, swa_sinks, b_merge,
     w_out, norm_ffn2, ffn2_w_in, ffn2_w_out) = lp
    B, T, _ = x.shape
    mod = (jax.nn.silu(c) @ w_ada + b_ada)[:, None, :]
    sh1, sc1, gt1, sh2, sc2, gt2, sh3, sc3, gt3 = jnp.split(mod, N_MOD, axis=-1)

    x = x + 0.5 * gt1 * swiglu(modulate(x, norm_ffn1, sh1, sc1), ffn1_w_in, ffn1_w_out)

    h = modulate(x, norm_mix, sh2, sc2)
    offsets = [int(o) for o in np.cumsum(IN_SIZES)[:-1]]
    conv_in, z, a, b, q_s, k_s, v_s, gate_logits = jnp.split(h @ w_in, offsets, axis=-1)

    conv_out, new_conv = causal_conv(conv_in, conv_prefix, gdn_conv_w)
    qg, kg, vg = jnp.split(conv_out, [GDN_QK_DIM, 2 * GDN_QK_DIM], axis=-1)
    qg = l2_norm(qg.reshape(B, T, GDN_HEADS, GDN_DK)) * (GDN_DK ** -0.5)
    kg = l2_norm(kg.reshape(B, T, GDN_HEADS, GDN_DK))
    vg = vg.reshape(B, T, GDN_HEADS, GDN_DV).astype(jnp.float32)
    g = -jnp.exp(gdn_a_log.astype(jnp.float32)) * jax.nn.softplus(a.astype(jnp.float32) + gdn_dt_bias.astype(jnp.float32))
    beta = jax.nn.sigmoid(b.astype(jnp.float32))
    o_g, new_S = gated_delta_chunked(qg, kg, vg, g, beta, S0.astype(jnp.float32))
    o_g = rms_norm(o_g, gdn_norm).astype(x.dtype).reshape(B, T, GDN_V_DIM) * jax.nn.silu(z)

    q_s = rms_norm(q_s.reshape(B, T, SWA_HQ, SWA_HD), swa_q_norm)
    k_s = rms_norm(k_s.reshape(B, T, SWA_HKV, SWA_HD), swa_k_norm)
    v_s = v_s.reshape(B, T, SWA_HKV, SWA_HD)
    if is_prompt:
        o_s = swa_prompt(q_s, k_s, v_s, swa_sinks)
        new_k, new_v = k_s[:, -WINDOW:], v_s[:, -WINDOW:]
    else:
        o_s, new_k, new_v = swa_sample(q_s, k_s, v_s, k_cache, v_cache, swa_sinks)

    g_a, g_b = jnp.split(jax.nn.sigmoid(gate_logits + b_merge), 2, axis=-1)
    x = x + gt2 * ((g_a * o_g + g_b * o_s) @ w_out)

    x = x + 0.5 * gt3 * swiglu(modulate(x, norm_ffn2, sh3, sc3), ffn2_w_in, ffn2_w_out)
    return x, new_conv, new_S, new_k, new_v


def setup_inputs(seed: int = 0) -> dict:
    key = jax.random.key(seed)
    ks = jax.random.split(key, 32)

    def nrm(k, shape, s):
        return jax.random.normal(k, shape, jnp.float32) * s

    dt = jnp.exp(jax.random.uniform(ks[17], (DEPTH, GDN_HEADS), jnp.float32, np.log(1e-3), np.log(1e-1)))
    return {
        'x_prompt': nrm(ks[0], (BATCH, SEQ, D_MODEL), 1.0),
        'x_sample': nrm(ks[1], (DEC_BATCH, DEC_SEQ, D_MODEL), 1.0),
        'state_gdn_conv': nrm(ks[2], (DEPTH, DEC_BATCH, CONV_W - 1, GDN_CONV_DIM), 1.0),
        'state_gdn': nrm(ks[3], (DEPTH, DEC_BATCH, GDN_HEADS, GDN_DK, GDN_DV), 0.1),
        'cache_swa_k': nrm(ks[4], (DEPTH, DEC_BATCH, WINDOW, SWA_HKV, SWA_HD), 1.0),
        'cache_swa_v': nrm(ks[5], (DEPTH, DEC_BATCH, WINDOW, SWA_HKV, SWA_HD), 1.0),
        'c_prompt': nrm(ks[6], (BATCH, D_MODEL), 1.0),
        'c_sample': nrm(ks[7], (DEC_BATCH, D_MODEL), 1.0),
        'w_ada': nrm(ks[8], (DEPTH, D_MODEL, N_MOD * D_MODEL), 0.3 * D_MODEL ** -0.5),
        'b_ada': nrm(ks[9], (DEPTH, N_MOD * D_MODEL), 0.01),
        'norm_ffn1': 1.0 + nrm(ks[10], (DEPTH, D_MODEL), 0.01),
        'ffn1_w_in': nrm(ks[11], (DEPTH, D_MODEL, 2 * D_FF), D_MODEL ** -0.5),
        'ffn1_w_out': nrm(ks[12], (DEPTH, D_FF, D_MODEL), D_FF ** -0.5),
        'norm_mix': 1.0 + nrm(ks[13], (DEPTH, D_MODEL), 0.01),
        'w_in': nrm(ks[14], (DEPTH, D_MODEL, IN_DIM), D_MODEL ** -0.5),
        'gdn_conv_w': nrm(ks[15], (DEPTH, CONV_W, GDN_CONV_DIM), CONV_W ** -0.5),
        'gdn_a_log': jnp.log(jax.random.uniform(ks[16], (DEPTH, GDN_HEADS), jnp.float32, 1.0, 16.0)),
        'gdn_dt_bias': dt + jnp.log(-jnp.expm1(-dt)),
        'gdn_norm': 1.0 + nrm(ks[18], (DEPTH, GDN_DV), 0.01),
        'swa_q_norm': 1.0 + nrm(ks[19], (DEPTH, SWA_HD), 0.01),
        'swa_k_norm': 1.0 + nrm(ks[20], (DEPTH, SWA_HD), 0.01),
        'swa_sinks': nrm(ks[21], (DEPTH, SWA_HQ), 0.5),
        'b_merge': nrm(ks[22], (DEPTH, 2 * D_MODEL), 0.01),
        'w_out': nrm(ks[23], (DEPTH, D_MODEL, D_MODEL), D_MODEL ** -0.5),
        'norm_ffn2': 1.0 + nrm(ks[24], (DEPTH, D_MODEL), 0.01),
        'ffn2_w_in': nrm(ks[25], (DEPTH, D_MODEL, 2 * D_FF), D_MODEL ** -0.5),
        'ffn2_w_out': nrm(ks[26], (DEPTH, D_FF, D_MODEL), D_FF ** -0.5),
    }


def reference(x_prompt, x_sample, state_gdn_conv, state_gdn, cache_swa_k, cache_swa_v, c_prompt, c_sample,
              w_ada, b_ada, norm_ffn1, ffn1_w_in, ffn1_w_out, norm_mix, w_in, gdn_conv_w, gdn_a_log,
              gdn_dt_bias, gdn_norm, swa_q_norm, swa_k_norm, swa_sinks, b_merge, w_out, norm_ffn2,
              ffn2_w_in, ffn2_w_out):
    yp, ys = x_prompt, x_sample
    conv_p, gdn_p, k_p, v_p = [], [], [], []
    conv_s, gdn_s, k_s, v_s = [], [], [], []
    for l in range(DEPTH):
        lp = (w_ada[l], b_ada[l], norm_ffn1[l], ffn1_w_in[l], ffn1_w_out[l], norm_mix[l], w_in[l],
              gdn_conv_w[l], gdn_a_log[l], gdn_dt_bias[l], gdn_norm[l], swa_q_norm[l], swa_k_norm[l],
              swa_sinks[l], b_merge[l], w_out[l], norm_ffn2[l], ffn2_w_in[l], ffn2_w_out[l])
        zero_conv = jnp.zeros((yp.shape[0], CONV_W - 1, GDN_CONV_DIM), yp.dtype)
        zero_S = jnp.zeros((yp.shape[0], GDN_HEADS, GDN_DK, GDN_DV), jnp.float32)
        yp, cp, sp, kp, vp = trunk_layer(yp, c_prompt, zero_conv, zero_S, None, None, lp, True)
        ys, cs, ss, kss, vss = trunk_layer(ys, c_sample, state_gdn_conv[l], state_gdn[l],
                                           cache_swa_k[l], cache_swa_v[l], lp, False)
        conv_p.append(cp); gdn_p.append(sp); k_p.append(kp); v_p.append(vp)
        conv_s.append(cs); gdn_s.append(ss); k_s.append(kss); v_s.append(vss)
    return (yp, ys,
            jnp.stack(conv_p), jnp.stack(gdn_p), jnp.stack(k_p), jnp.stack(v_p),
            jnp.stack(conv_s), jnp.stack(gdn_s), jnp.stack(k_s), jnp.stack(v_s))
```

```python
import numpy as np
from contextlib import ExitStack
import concourse.bass as bass
import concourse.mybir as mybir
from concourse.bass_utils import run_bass_kernel_spmd

F32 = mybir.dt.float32
BF16 = mybir.dt.bfloat16
AF = mybir.ActivationFunctionType
ALU = mybir.AluOpType

NCORES = 8
D = 1024
KC = 8
DFF = 2816
FC = 22
NMOD = 9
NSEQ = 6
TP = 2048
TS = 64
NTOK = 2 * TP + 4 * TS
EPS = 1e-6

ENGS = ['pe', 'act', 'dve', 'pool', 'sp']
DMA_ENGS = ('sp',)


class Op:
    __slots__ = ('eng', 'fn', 'deps', 'needed', 'count', 'dma', 'slot', 'val', 'idx', 'vc')


class Prog:
    def __init__(self, nc, n_dma_slots=16):
        self.nc = nc
        self.ops = {e: [] for e in ENGS}
        self.lastw = {}
        self.readers = {}
        self.K = n_dma_slots

    def op(self, eng, fn, reads=(), writes=()):
        o = Op()
        o.eng = eng
        o.fn = fn
        o.idx = len(self.ops[eng])
        o.needed = False
        o.dma = eng in DMA_ENGS or getattr(fn, 'is_dma', False)
        o.slot = None
        deps = set()
        for r in reads:
            w = self.lastw.get(r)
            if w is not None:
                deps.add(w)
            if r[0] == 'ps':
                for rd in self.readers.get(r, ()):
                    if rd[0] != eng:
                        deps.add(rd)
        for wr in writes:
            w = self.lastw.get(wr)
            if w is not None:
                deps.add(w)
            for rd in self.readers.get(wr, ()):
                deps.add(rd)
        if eng == 'pe':
            deps = {d for d in deps if d[0] != 'pe'}
        deps.discard((eng, o.idx))
        prev = self.ops[eng][-1] if self.ops[eng] else None
        vc = dict(prev.vc) if prev is not None else {}
        infos = []
        for d in deps:
            dop = self.ops[d[0]][d[1]]
            after = dict(dop.vc)
            if not dop.dma:
                if after.get(d[0], -1) < d[1]:
                    after[d[0]] = d[1]
            infos.append((d, dop, after))
        keep = set()
        for (d, dop, after) in infos:
            if dop.dma:
                keep.add(d)
                continue
            f, i = d
            implied = vc.get(f, -1) >= i
            if not implied:
                for (d2, dop2, after2) in infos:
                    if d2 != d and after2.get(f, -1) >= i:
                        implied = True
                        break
            if not implied:
                keep.add(d)
        for (d, dop, after) in infos:
            for f, i in after.items():
                if vc.get(f, -1) < i:
                    vc[f] = i
        o.vc = vc
        self.n_pruned = getattr(self, 'n_pruned', 0) + (len(deps) - len(keep))
        self.n_kept = getattr(self, 'n_kept', 0) + len(keep)
        deps = keep
        o.deps = deps
        me = (eng, o.idx)
        for r in reads:
            self.readers.setdefault(r, []).append(me)
        for wr in writes:
            self.lastw[wr] = me
            self.readers[wr] = []
        self.ops[eng].append(o)
        return o

    def finalize(self):
        print("ops:", {e: len(v) for e, v in self.ops.items()}, "deps kept", self.n_kept, "pruned", self.n_pruned)
        for e in ENGS:
            for o in self.ops[e]:
                for (f, i) in o.deps:
                    self.ops[f][i].needed = True
        for e in ENGS:
            k = 0
            c = 0
            for o in self.ops[e]:
                if o.dma:
                    o.slot = k % self.K
                    o.val = 16 * (k // self.K + 1)
                    k += 1
                else:
                    if o.needed:
                        c += 1
                o.count = c

    def emit_engine(self, eng, e, sems, dsems):
        waited = {}

        def wait(key, sem, val):
            if waited.get(key, 0) >= val:
                return
            waited[key] = val
            e.wait_ge(sem, val)

        last = {}
        for o in self.ops[eng]:
            for (f, i) in sorted(o.deps):
                d = self.ops[f][i]
                if d.dma:
                    wait((f, 'd', d.slot), dsems[f][d.slot], d.val)
                else:
                    wait(f, sems[f], d.count)
            if o.dma:
                if o.val > 16:
                    wait((eng, 'd', o.slot), dsems[eng][o.slot], o.val - 16)
                ins = o.fn(e)
                ins.then_inc(dsems[eng][o.slot], 16)
                last[o.slot] = o.val
            else:
                ins = o.fn(e)
                if o.needed:
                    ins.then_inc(sems[eng], 1)
        for s, v in last.items():
            wait((eng, 'd', s), dsems[eng][s], v)

    def emit(self, es):
        nc = self.nc
        self.finalize()
        sems = {e: es.enter_context(nc.semaphore("sem_" + e)) for e in ENGS if e not in DMA_ENGS}
        dsems = {e: [es.enter_context(nc.semaphore("dsem_%s_%d" % (e, k))) for k in range(self.K)]
                 for e in ('sp', 'pool')}
        block = es.enter_context(nc.Block())

        @block.tensor
        def _(e):
            self.emit_engine('pe', e, sems, dsems)

        @block.scalar
        def _(e):
            self.emit_engine('act', e, sems, dsems)

        @block.vector
        def _(e):
            self.emit_engine('dve', e, sems, dsems)

        @block.gpsimd
        def _(e):
            self.emit_engine('pool', e, sems, dsems)

        @block.sync
        def _(e):
            self.emit_engine('sp', e, sems, dsems)


class K:
    def __init__(self, cfg):
        self.cfg = cfg
        nc = self.nc = bass.Bass("TRN2", target_bir_lowering=False)
        self.es = ExitStack()
        self.P = Prog(nc)
        self.dram = {}
        self.sb = {}
        self._psrr = 0

    def din(self, name, shape, dt=F32):
        t = self.nc.dram_tensor(name, list(shape), dt, kind="ExternalInput").ap()
        self.dram[name] = t
        return t

    def dout(self, name, shape, dt=F32):
        t = self.nc.dram_tensor(name, list(shape), dt, kind="ExternalOutput").ap()
        self.dram[name] = t
        return t

    def dscr(self, name, shape, dt):
        t = self.nc.dram_tensor(name, list(shape), dt).ap()
        self.dram[name] = t
        return t

    def tile(self, name, shape, dt):
        t = self.es.enter_context(self.nc.sbuf_tensor("sb_" + name, list(shape), dt))
        self.sb[name] = t
        return t

    def psum(self, name, shape, dt=F32):
        return self.es.enter_context(self.nc.psum_tensor(name, list(shape), dt))

    def dma(self, out, in_, reads, writes, eng='sp'):
        def fn(e):
            return e.dma_start(out=out, in_=in_)
        fn.is_dma = True
        return self.P.op(eng, fn, reads, writes)

    def mm_group(self, out, pairs, reads, writes):
        n = len(pairs)

        def fn(e):
            ins = None
            for i, (l, r) in enumerate(pairs):
                ins = e.matmul(out, l, r, start=(i == 0), stop=(i == n - 1))
            return ins
        return self.P.op('pe', fn, reads, writes)

    def act(self, out, in_, func, reads, writes, bias=None, scale=None):
        def fn(e):
            kw = {}
            if bias is not None:
                kw['bias'] = bias
            if scale is not None:
                kw['scale'] = scale
            return e.activation(out=out, in_=in_, func=func, **kw)
        return self.P.op('act', fn, reads, writes)

    def tt(self, eng, out, in0, in1, op, reads, writes):
        def fn(e):
            return e.tensor_tensor(out=out, in0=in0, in1=in1, op=op)
        return self.P.op(eng, fn, reads, writes)

    def ts(self, eng, out, in0, s1, s2, op0, op1, reads, writes):
        def fn(e):
            if op1 is None:
                return e.tensor_scalar(out=out, in0=in0, scalar1=s1, scalar2=None, op0=op0)
            return e.tensor_scalar(out=out, in0=in0, scalar1=s1, scalar2=s2, op0=op0, op1=op1)
        return self.P.op(eng, fn, reads, writes)

    def stt(self, eng, out, in0, scalar, in1, op0, op1, reads, writes):
        def fn(e):
            return e.scalar_tensor_tensor(out=out, in0=in0, scalar=scalar, in1=in1, op0=op0, op1=op1)
        return self.P.op(eng, fn, reads, writes)

    def copy(self, eng, out, in_, reads, writes):
        def fn(e):
            return e.tensor_copy(out=out, in_=in_)
        return self.P.op(eng, fn, reads, writes)

    def memset(self, eng, ap, val, writes):
        def fn(e):
            return e.memset(ap, val)
        return self.P.op(eng, fn, (), writes)


def host_group_weights(w, col_groups, kc):
    out = []
    for cols in col_groups:
        blk = w[:, cols]
        blk = blk.reshape(kc, 128, len(cols)).transpose(1, 0, 2)
        out.append(blk)
    return np.ascontiguousarray(np.stack(out, 0))


def ffn_in_groups():
    gs = []
    for g in range(FC // 2):
        cols = np.concatenate([np.arange(2 * g * 128, (2 * g + 2) * 128),
                               DFF + np.arange(2 * g * 128, (2 * g + 2) * 128)])
        gs.append(cols)
    return gs


def ffn_out_groups():
    return [np.arange(oc * 128, (oc + 1) * 128) for oc in range(KC)]


NPROJ = 64
PP_CW, PP_BM, PP_GN, PP_QG, PP_KG, NPP = 0, 96, 112, 113, 114, 115


def build(cfg):
    kb = K(cfg)
    nc, P = kb.nc, kb.P
    tiles = cfg['tiles']
    do_mixer = cfg.get('mixer', True)
    STOP = cfg.get('stop', 99)

    xT = kb.din("xT", [D, NTOK])
    yT = kb.dout("yT", [D, NTOK])
    cT = kb.din("cT", [128, KC, NSEQ])
    w_ada = kb.din("w_ada", [128, KC, NMOD * D])
    b_ada = kb.din("b_ada", [128, NMOD * KC])
    gains = kb.din("gains", [128, 3, KC])
    wsrc = {
        'f1i': kb.din("f1i", [FC // 2, 128, KC, 512]),
        'f1o': kb.din("f1o", [KC, 128, FC, 128]),
        'f2i': kb.din("f2i", [FC // 2, 128, KC, 512]),
        'f2o': kb.din("f2o", [KC, 128, FC, 128]),
        'win': kb.din("win", [NPROJ // 4, 128, KC, 512]),
        'wout': kb.din("wout", [2, 128, KC, 512]),
    }
    wscr = {k: kb.dscr(k + "_bf", list(v.shape), BF16) for k, v in wsrc.items()}
    wab_d = kb.din("wab", [128, KC, 16])
    c32_d = kb.din("c32", [64, 5, 64])
    id128_d = kb.din("id128", [128, 2, 128])
    pp_d = kb.din("pp", [128, NPP])
    tb_d = kb.din("tb", [64, 16])
    snk_d = kb.din("snk", [128, 16])
    Sin_d = kb.din("Sin", [128, 4, 8, 128])
    histin_d = kb.din("histin", [128, 24, 4, 3])
    kcache_d = kb.din("kcache", [128, 4, 4, 128])
    vcache_d = kb.din("vcache", [128, 4, 4, 128])
    vcachetm_d = kb.din("vcachetm", [64, 4, 4, 2, 128])
    Sout_d = kb.dout("Sout", [128, NSEQ, 8, 128])
    histout_d = kb.dout("histout", [128, 24, NSEQ, 3])
    kout_d = kb.dout("kout", [128, NSEQ, 4, 128])
    vout_d = kb.dout("vout", [128, NSEQ, 4, 128])

    ones = kb.tile("ones", [128, 128], BF16)
    ones1 = kb.tile("ones1", [128, 128], BF16)
    ones128 = kb.tile("ones128", [128, 128], BF16)
    onesf = kb.tile("onesf", [64, 128], F32)
    epsc = kb.tile("epsc", [128, 1], F32)
    lnq = kb.tile("lnq", [128, 1], F32)
    onec = kb.tile("onec", [128, 1], F32)
    x_sb = kb.tile("x_sb", [128, KC, 512], F32)
    sq = kb.tile("sq", [128, KC, 512], BF16)
    tmp = kb.tile("tmp", [128, KC, 512], BF16)
    rstd = kb.tile("rstd", [128, 512], F32)
    hT = kb.tile("hT", [128, KC, 512], BF16)
    hid = kb.tile("hid", [128, FC, 512], BF16)
    sg = [kb.tile("sg%d" % i, [128, 512], F32) for i in range(2)]
    NWB = 3
    wbuf = [kb.tile("wbuf%d" % i, [128, 4096], BF16) for i in range(NWB)]
    arena = kb.tile("arena", [128, 8192], F32)
    stg32 = [arena[:, 0:4096]]
    stg16 = [arena[:, 4096:6144].bitcast(BF16)]
    wada_sb = [arena[:, 0:4096].rearrange("p (k n) -> p k n", k=KC),
               arena[:, 4096:8192].rearrange("p (k n) -> p k n", k=KC)]
    ab16 = arena[:, :].bitcast(BF16)
    TWm = 256
    qkT = ab16[:, 0:4096].rearrange("p (f t) -> p f t", f=16)
    vT = ab16[:, 4096:6144].rearrange("p (f t) -> p f t", f=8)
    zT = ab16[:, 6144:8192].rearrange("p (f t) -> p f t", f=8)
    qsT = ab16[:, 8192:10240].rearrange("p (f t) -> p f t", f=8)
    gates = ab16[:, 10240:14336].rearrange("p (f t) -> p f t", f=16)
    osT = ab16[:, 14336:16384].rearrange("p (f t) -> p f t", f=8)
    cT_sb = kb.tile("cT_sb", [128, KC, NSEQ], F32)
    scT = kb.tile("scT", [128, KC, NSEQ], F32)
    bada_sb = kb.tile("bada_sb", [128, NMOD * KC], F32)
    gains_sb = kb.tile("gains_sb", [128, 3, KC], F32)
    modT = kb.tile("modT", [128, NMOD * KC, NSEQ], F32)
    Amod = kb.tile("Amod", [128, 3, KC, NSEQ], F32)
    Gmod = kb.tile("Gmod", [128, 3, KC, NSEQ], F32)
    wab32 = kb.tile("wab32", [128, KC, 16], F32)
    wab = kb.tile("wabb", [128, KC, 16], BF16)
    c32 = kb.tile("c32", [64, 5, 64], F32)
    id32 = kb.tile("id32", [128, 2, 128], F32)
    idb = kb.tile("idb", [128, 2, 128], BF16)
    pp = kb.tile("pp", [128, NPP], F32)
    tb = kb.tile("tb", [64, 16], F32)
    negA = kb.tile("negA", [64, 8], F32)
    snk = kb.tile("snk", [128, 16], F32)
    esnk = kb.tile("esnk", [128, 16], F32)
    qg8 = kb.tile("qg8", [128, 1], F32)
    hist = kb.tile("hist", [128, 24, NSEQ, 3], F32)
    rawb = [kb.tile("rawb%d" % i, [128, 4 * 67], F32) for i in range(2)]
    cacc = [kb.tile("cacc%d" % i, [128, 256], F32) for i in range(2)]
    sqb = [kb.tile("sqb%d" % i, [128, 512], BF16) for i in range(2)]
    kn32 = [kb.tile("kn32_%d" % i, [128, 512], F32) for i in range(2)]
    rs32 = [kb.tile("rs32_%d" % i, [128, 512], F32) for i in range(2)]
    ksT = kb.tile("ksT", [128, 4, 4 * (128 + 64)], BF16)
    vsb = kb.tile("vsb", [128, 4, 256], BF16)
    vtm = kb.tile("vtm", [64, 4, 4 * 3 * 128], BF16)
    S32 = kb.tile("S32", [128, 8, 128], F32)
    Sbf = [kb.tile("Sbf%d" % i, [128, 8, 128], BF16) for i in range(2)]
    oTb = tmp[:].rearrange("p k t -> p (k t)").bitcast(F32).rearrange("p (h t) -> p h t", h=8)

    def hidf32(k0, nk):
        return hid[:, k0:k0 + nk, :].rearrange("p k t -> p (k t)").bitcast(F32)
    kvst = hidf32(0, 4)
    INb = hid[0:64, 12, :]
    Tn = hid[0:64, 13, :]
    Fb = hid[0:64, 14, :]
    Cb = hid[:, 15, :]
    gs = {n: kb.tile("gs_" + n, [64, 4, 8], F32) for n in ('xa', 'ea', 'g', 'eb', 'beta', 'Gs', 'eG', 'bG', 'dec', 'dd')}
    dlast = kb.tile("dlast", [128, 4, 8], F32)
    Pc = hidf32(4, 2)[0:64, :].rearrange("p (h j) -> p h j", h=8)
    Em = hidf32(6, 2)[0:64, :]
    tA = [kb.tile("tA%d" % i, [64, 512], F32) for i in range(2)]
    BS = hidf32(8, 2)[0:64, :].rearrange("p (h j) -> p h j", h=8)
    Nb = [kb.tile("Nb%d" % i, [64, 512], BF16) for i in range(2)]
    Mb = [kb.tile("Mb%d" % i, [64, 512], BF16) for i in range(2)]
    Rb = [kb.tile("Rb%d" % i, [128, 512], BF16) for i in range(2)]
    QKb = kb.tile("QKb", [64, 512], BF16)
    QKT = kb.tile("QKT", [128, 512], BF16)
    vbt = kb.tile("vbt", [128, 8, 128], BF16)
    kbt = kb.tile("kbt", [64, 8, 128], BF16)
    kdt = kb.tile("kdt", [64, 8, 128], BF16)
    negwT = kb.tile("negwT", [128, 512], BF16)
    qdecT = kb.tile("qdecT", [128, 512], BF16)
    rhsE = hidf32(10, 2)[0:64, :].rearrange("p (h j) -> p h j", h=8)
    vnew = kb.tile("vnew", [128, 8, 128], BF16)
    pT = [kb.tile("pT%d" % i, [64, 768], BF16) for i in range(2)]
    rden = [kb.tile("rden%d" % i, [128, 256], F32) for i in range(2)]
    mg = kb.tile("mg", [128, 256], F32)

    print('SBUF bytes remaining per partition:', nc.sbuf_bytes_remaining)
    NPS = 8
    ps = [kb.psum("ps%d" % i, [128, 512]) for i in range(NPS)]
    rr = {'i': 0}

    def next_ps():
        b = rr['i'] % NPS
        rr['i'] += 1
        key = ('ps', b)
        if key in P.lastw and not P.readers.get(key):
            raise RuntimeError("PSUM bank %d re-allocated while its last result is unconsumed" % b)
        return b

    AR = ('arena',)

    kb.memset('pool', ones[:], 1.0 / 1024.0, [('ones',)])
    kb.memset('pool', ones1[:], 1.0, [('ones',)])
    kb.memset('pool', ones128[:], 1.0 / 128.0, [('ones',)])
    kb.memset('pool', onesf[:], 1.0, [('ones',)])
    kb.memset('pool', epsc[:], EPS, [('epsc',)])
    kb.memset('pool', lnq[:], float(np.log(128.0 ** -0.5)), [('epsc',)])
    kb.memset('pool', onec[:], 1.0, [('epsc',)])
    kb.memset('pool', hist[:], 0.0, [('hist',)])
    kb.memset('pool', Rb[0][:], 0.0, [('Rb', 0, 0), ('Rb', 0, 1)])
    kb.memset('pool', Rb[1][:], 0.0, [('Rb', 1, 0), ('Rb', 1, 1)])
    kb.memset('pool', QKT[:], 0.0, [('QKT', 0), ('QKT', 1)])
    kb.memset('pool', vbt[:], 0.0, [('vbt', 0), ('vbt', 1)])
    kb.memset('pool', vnew[:], 0.0, [('vnew', 0), ('vnew', 1)])
    worder = ([('f1i', g) for g in range(FC // 2)] + [('f1o', g) for g in range(KC)]
              + ([('win', g) for g in range(NPROJ // 4)] + [('wout', g) for g in range(2)] if do_mixer else [])
              + [('f2i', g) for g in range(FC // 2)] + [('f2o', g) for g in range(KC)])
    wpos = {k: i for i, k in enumerate(worder)}
    cast_done = {'n': 0}
    LOOK = 10

    def ensure_cast(upto):
        while cast_done['n'] < min(upto + 1, len(worder)):
            name, g = worder[cast_done['n']]
            cast_done['n'] += 1
            kb.dma(wscr[name][g].rearrange("p k n -> p (k n)"), wsrc[name][g].rearrange("p k n -> p (k n)"),
                   [], [('wscr', name, g)], eng='pool')

    ensure_cast(FC // 2 + KC - 1)
    kb.dma(cT_sb[:], cT[:, :, :], [], [('cT',)])
    kb.dma(bada_sb[:], b_ada[:, :], [], [('bada',)])
    kb.dma(gains_sb[:], gains[:, :, :], [], [('gains',)])
    kb.dma(wab32[:], wab_d[:, :, :], [], [('wab32',)])
    kb.dma(c32[:], c32_d[:, :, :], [], [('c32',)])
    kb.dma(id32[:], id128_d[:, :, :], [], [('id32',)])
    kb.dma(pp[:], pp_d[:, :], [], [('pp',)])
    kb.dma(tb[:], tb_d[:, :], [], [('tb',)])
    kb.dma(snk[:], snk_d[:, :], [], [('snk',)])
    kb.dma(hist[:, :, 2:6, :], histin_d[:, :, :, :], [('hist',)], [('hist',)])
    kb.copy('pool', wab[:], wab32[:], [('wab32',)], [('wab',)])
    kb.copy('pool', idb[:], id32[:], [('id32',)], [('idb',)])
    kb.ts('dve', qg8[:], pp[:, PP_QG:PP_QG + 1], 0.125, None, ALU.mult, None, [('pp',)], [('qg8',)])
    kb.act(negA[:], tb[:, 8:16], AF.Exp, [('tb',)], [('negA',)])
    kb.ts('dve', negA[:], negA[:], -1.0, None, ALU.mult, None, [('negA',)], [('negA',)])
    kb.act(esnk[:], snk[:], AF.Exp, [('snk',)], [('esnk',)])
    kb.dma(kout_d[:, 2:6, :, 0:64], kcache_d[:, :, :, 64:128], [], [('kout', 'c')])
    kb.dma(vout_d[:, 2:6, :, 0:64], vcache_d[:, :, :, 64:128], [], [('vout', 'c')])
    kb.act(scT[:], cT_sb[:], AF.Silu, [('cT',)], [('scT',)])
    NG = NMOD * D // 512
    for g in range(NG):
        wb = wada_sb[g % 2]
        kb.dma(wb, w_ada[:, :, g * 512:(g + 1) * 512], [AR], [('wada', g % 2)])
        b = next_ps()
        kb.mm_group(ps[b][0:NSEQ, 0:512], [(scT[:, kc, :], wb[:, kc, :]) for kc in range(KC)],
                    [('wada', g % 2), ('scT',), AR], [('ps', b)])
        mtm = sg[g % 2][0:NSEQ, :]
        kb.act(mtm, ps[b][0:NSEQ, 0:512], AF.Copy, [('ps', b)], [('sg', g % 2)])
        b2 = next_ps()
        for j in range(4):
            kb.mm_group(ps[b2][:, j * NSEQ:(j + 1) * NSEQ], [(mtm[:, j * 128:(j + 1) * 128], c32[0:NSEQ, 3, 0:NSEQ])],
                        [('sg', g % 2), ('c32',)], [('ps', b2)])
        kb.tt('dve', modT[:, g * 4:(g + 1) * 4, :],
              ps[b2][:, 0:4 * NSEQ].rearrange("p (j s) -> p j s", j=4),
              bada_sb[:, g * 4:(g + 1) * 4].unsqueeze(2).to_broadcast([128, 4, NSEQ]),
              ALU.add, [('ps', b2), ('bada',)], [('modT',)])
    for i in range(3):
        sc = modT[:, (3 * i + 1) * KC:(3 * i + 2) * KC, :]
        gt = modT[:, (3 * i + 2) * KC:(3 * i + 3) * KC, :]
        kb.stt('dve', Amod[:, i, :, :], sc, 1.0,
               gains_sb[:, i, :].unsqueeze(2).to_broadcast([128, KC, NSEQ]),
               ALU.add, ALU.mult, [('modT',), ('gains',)], [('Amod',)])
        kb.ts('dve', Gmod[:, i, :, :], gt, 0.5 if i != 1 else 1.0, None, ALU.mult, None,
              [('modT',)], [('Gmod',)])

    junk = kb.tile("junk", [128, 4], F32)
    kb.memset('dve', junk[:, 0:1], 0.0, [AR])
    kb.memset('pool', junk[:, 1:2], 0.0, [AR])
    kb.act(junk[:, 2:3], epsc[:], AF.Copy, [('epsc',)], [AR])

    wrr = {'i': 0}
    def load_w(name, g, kc, ncols):
        ensure_cast(wpos[(name, g)] + LOOK)
        s = wrr['i'] % NWB
        wrr['i'] += 1
        kb.dma(wbuf[s][:, 0:kc * ncols], wscr[name][g].rearrange("p k n -> p (k n)"),
               [('wscr', name, g)], [('wbuf', s)])
        return s, wbuf[s][:, 0:kc * ncols].rearrange("p (k n) -> p k n", k=kc)

    def rsqrt_from_ps(dst, b, W, bias_ap=None, rd=(), npart=128):
        kb.act(dst, ps[b][0:npart, 0:W], AF.Ln, [('ps', b), ('epsc',)], list(rd), bias=epsc[0:npart, 0:1])
        if bias_ap is None:
            kb.act(dst, dst, AF.Exp, list(rd), list(rd), scale=-0.5)
        else:
            kb.act(dst, dst, AF.Exp, list(rd) + [('epsc',)], list(rd), scale=-0.5, bias=bias_ap)

    def norm_mod(i, c0, W, segs, alt=False):
        so = 256 if alt else 0
        ho = 256 if alt else 0
        hk = 'hT2' if alt else 'hT'
        rk = ('rstd2',) if alt else ('rstd',)
        tbuf = sq[:, :, 0:W] if alt else tmp[:, :, 0:W]
        tk_ = ('sq',) if alt else ('tmp',)
        kb.act(sq[:, :, so:so + W], x_sb[:, :, c0:c0 + W], AF.Square, [('x',)], [('sq',)])
        b = next_ps()
        pairs = [(ones[:], sq[:, kc, so:so + W]) for kc in range(KC)]
        kb.mm_group(ps[b][:, 0:W], pairs, [('ones',), ('sq',)], [('ps', b)])
        rsqrt_from_ps(rstd[:, so:so + W], b, W, rd=[rk])
        kb.tt('dve', tbuf, x_sb[:, :, c0:c0 + W],
              rstd[:, so:so + W].unsqueeze(1).to_broadcast([128, KC, W]), ALU.mult,
              [('x',), rk], [tk_])
        sh0 = 3 * i * KC
        for kc in range(KC):
            for (a, b_, s) in segs:
                kb.act(hT[:, kc, ho + a:ho + b_], tbuf[:, kc, a:b_], AF.Identity,
                       [tk_, ('Amod',), ('modT',)], [(hk, kc)],
                       bias=modT[:, sh0 + kc, s:s + 1], scale=Amod[:, i, kc, s:s + 1])

    def ffn(i, wi, wo, TW, segs):
        norm_mod(i, 0, TW, segs)
        for g in range(FC // 2):
            s, w = load_w(wi, g, KC, 512)
            for j in range(2):
                fc = 2 * g + j
                bg = next_ps()
                bu = next_ps()
                kb.mm_group(ps[bg][:, 0:TW], [(w[:, kc, j * 128:(j + 1) * 128], hT[:, kc, 0:TW]) for kc in range(KC)],
                            [('wbuf', s)] + [('hT', kc) for kc in range(KC)], [('ps', bg)])
                kb.mm_group(ps[bu][:, 0:TW], [(w[:, kc, 256 + j * 128:256 + (j + 1) * 128], hT[:, kc, 0:TW]) for kc in range(KC)],
                            [('wbuf', s)] + [('hT', kc) for kc in range(KC)], [('ps', bu)])
                sgt = sg[fc % 2]
                kb.act(sgt[:, 0:TW], ps[bg][:, 0:TW], AF.Silu, [('ps', bg)], [('sg', fc % 2)])
                kb.tt('dve', hid[:, fc, 0:TW], sgt[:, 0:TW], ps[bu][:, 0:TW], ALU.mult,
                      [('sg', fc % 2), ('ps', bu)], [('hid', fc)])
        for oc in range(KC):
            s, w = load_w(wo, oc, FC, 128)
            b = next_ps()
            kb.mm_group(ps[b][:, 0:TW], [(w[:, fc, :], hid[:, fc, 0:TW]) for fc in range(FC)],
                        [('wbuf', s)] + [('hid', fc) for fc in range(FC)], [('ps', b)])
            for (c0, c1, sq_) in segs:
                kb.stt('dve', x_sb[:, oc, c0:c1], ps[b][:, c0:c1], Gmod[:, i, oc, sq_:sq_ + 1],
                       x_sb[:, oc, c0:c1], ALU.mult, ALU.add,
                       [('ps', b), ('Gmod',), ('x',)], [('x',)])

    def mixer(m0, nseg, L, seq0, gch0, first, last, prenormed=False, after_proj=None):
        W = nseg * L
        NCH = W // 64
        cps = L // 64
        segs = [(j * L, (j + 1) * L, seq0 + j) for j in range(nseg)]
        HK = 128 + L
        HV = 2 + cps
        if first and nseg == 1:
            kb.memset('pool', S32[:], 0.0, [('S32', 0), ('S32', 1)])
            kb.memset('pool', Sbf[0][:], 0.0, [('Sbf', 0, 0), ('Sbf', 0, 1)])
        if nseg == 4:
            for j in range(4):
                st = kvst[:, 0:512].rearrange("p (s t) -> p s t", s=4)
                kb.dma(st, kcache_d[:, :, j, :], [], [('hid', 0), ('hid', 1), ('hid', 2), ('hid', 3)])
                kb.copy('pool', ksT[:, j, :].rearrange("p (s t) -> p s t", s=4)[:, :, 0:128], st,
                        [('hid', 0), ('hid', 1), ('hid', 2), ('hid', 3)], [('ksT', j)])
                st2 = kvst[0:64, 0:1024].rearrange("p (s c d) -> p s c d", s=4, c=2)
                kb.dma(st2, vcachetm_d[:, j, :, :, :], [], [('hid', 0), ('hid', 1), ('hid', 2), ('hid', 3)])
                kb.copy('pool', vtm[:, j, :].rearrange("p (s c d) -> p s c d", s=4, c=HV)[:, :, 0:2, :], st2,
                        [('hid', 0), ('hid', 1), ('hid', 2), ('hid', 3)], [('vtm', j)])
        if not prenormed:
            norm_mod(1, m0, W, segs)
        ho = 256 if prenormed else 0
        hk = 'hT2' if prenormed else 'hT'
        seqsl = slice(seq0, seq0 + nseg)
        pend = []
        for g in range(NPROJ // 4):
            s, w = load_w('win', g, KC, 512)
            for jj in range(4):
                ch = 4 * g + jj
                b = next_ps()
                kb.mm_group(ps[b][:, 0:W], [(w[:, kc, jj * 128:(jj + 1) * 128], hT[:, kc, ho:ho + W]) for kc in range(KC)],
                            [('wbuf', s)] + [(hk, kc) for kc in range(KC)], [('ps', b)])
                pv = ps[b][:, 0:W]
                while pend and pend[0][0] <= ch - 2:
                    pend.pop(0)[1]()
                if ch < 24:
                    r = ch % 2
                    rb = rawb[r][:, 0:nseg * (3 + L)].rearrange("p (s t) -> p s t", s=nseg)
                    kb.copy('pool', rb[:, :, 0:3], hist[:, ch, seqsl, :], [('hist',)], [('rawb', r)])
                    kb.copy('dve', rb[:, :, 3:3 + L], pv.rearrange("p (s t) -> p s t", s=nseg),
                            [('ps', b)], [('rawb', r)])
                    kb.copy('pool', hist[:, ch, seqsl, :], rb[:, :, L:L + 3], [('rawb', r)], [('hist',)])
                    acc = cacc[r][:, 0:W].rearrange("p (s t) -> p s t", s=nseg)
                    kb.act(acc, pv.rearrange("p (s t) -> p s t", s=nseg), AF.Copy, [('ps', b), ('pp',)], [('cacc', r)],
                           scale=pp[:, PP_CW + ch * 4 + 3:PP_CW + ch * 4 + 4])
                    for tap in (2, 1, 0):
                        kb.stt('dve', acc, rb[:, :, tap:tap + L], pp[:, PP_CW + ch * 4 + tap:PP_CW + ch * 4 + tap + 1],
                               acc, ALU.mult, ALU.add, [('rawb', r), ('pp',), ('cacc', r)], [('cacc', r)])
                    dst = qkT[:, ch, 0:W] if ch < 16 else vT[:, ch - 16, 0:W]
                    kb.act(dst, cacc[r][:, 0:W], AF.Silu, [('cacc', r)], [('qkv', ch)])
                elif ch < 32:
                    kb.act(zT[:, ch - 24, 0:W], pv, AF.Silu, [('ps', b)], [('zT', ch - 24)])
                elif ch < 44:
                    r = (ch // 2) % 2
                    hf = ch % 2
                    kb.act(sqb[r][:, hf * W:(hf + 1) * W], pv, AF.Square, [('ps', b)], [('sqb', r)])
                    kb.copy('dve', kn32[r][:, hf * W:(hf + 1) * W], pv, [('ps', b)], [('kn32', r)])
                    if hf == 1:
                        def fin(ch0=ch - 1, r=r):
                            b2 = next_ps()
                            for hh_ in range(2):
                                kb.mm_group(ps[b2][:, hh_ * W:(hh_ + 1) * W], [(idb[:, 1, :], sqb[r][:, hh_ * W:(hh_ + 1) * W])],
                                            [('idb',), ('sqb', r)], [('ps', b2)])
                            rsqrt_from_ps(rs32[r][:, 0:2 * W], b2, 2 * W, rd=[('rs32', r)])
                            if ch0 < 40:
                                c0_ = ch0 - 32
                                kb.stt('dve', qsT[:, c0_:c0_ + 2, 0:W], kn32[r][:, 0:2 * W].rearrange("p (a t) -> p a t", a=2),
                                       qg8[:, 0:1], rs32[r][:, 0:2 * W].rearrange("p (a t) -> p a t", a=2),
                                       ALU.mult, ALU.mult, [('kn32', r), ('rs32', r), ('qg8',)],
                                       [('qsT', c0_), ('qsT', c0_ + 1)])
                            else:
                                kb.stt('dve', kn32[r][:, 0:2 * W], kn32[r][:, 0:2 * W], pp[:, PP_KG:PP_KG + 1], rs32[r][:, 0:2 * W],
                                       ALU.mult, ALU.mult, [('kn32', r), ('rs32', r), ('pp',)], [('kn32', r)])
                                for hh_ in range(2):
                                    j = ch0 - 40 + hh_
                                    src = kn32[r][:, hh_ * W:(hh_ + 1) * W]
                                    kb.copy('pool', ksT[:, j, 0:nseg * HK].rearrange("p (s t) -> p s t", s=nseg)[:, :, 128:128 + L],
                                            src.rearrange("p (s t) -> p s t", s=nseg), [('kn32', r)], [('ksT', j)])
                                    if nseg == 4:
                                        kb.dma(kout_d[:, 2:6, j, 64:128], src.rearrange("p (s t) -> p s t", s=4),
                                               [('kn32', r)], [('kout', j)])
                                    elif last:
                                        kb.dma(kout_d[:, seq0, j, :], src[:, L - 128:L], [('kn32', r)], [('kout', seq0, j)])
                        pend.append((ch, fin))
                elif ch < 48:
                    j = ch - 44
                    kb.act(vsb[:, j, 0:W], pv, AF.Copy, [('ps', b)], [('vsb', j)])
                    if nseg == 4 or last:
                        r = ch % 2
                        kb.copy('dve', kn32[r][:, 256:256 + W], pv, [('ps', b)], [('kn32', r)])
                        if nseg == 4:
                            kb.dma(vout_d[:, 2:6, j, 64:128], kn32[r][:, 256:256 + W].rearrange("p (s t) -> p s t", s=4),
                                   [('kn32', r)], [('vout', j)])
                        else:
                            kb.dma(vout_d[:, seq0, j, :], kn32[r][:, 256 + L - 128:256 + L], [('kn32', r)], [('vout', seq0, j)])
                else:
                    f = ch - 48
                    kb.act(gates[:, f, 0:W], pv, AF.Sigmoid, [('ps', b), ('pp',)], [('gates', f)],
                           bias=pp[:, PP_BM + f:PP_BM + f + 1])
        while pend:
            pend.pop(0)[1]()
        if after_proj is not None:
            after_proj()
        if STOP <= 0:
            return
        pab = next_ps()
        for c in range(NCH):
            kb.mm_group(ps[pab][0:64, c * 16:(c + 1) * 16],
                        [(hT[:, kc, ho + c * 64:ho + (c + 1) * 64], wab[:, kc, :]) for kc in range(KC)],
                        [('wab',)] + [(hk, kc) for kc in range(KC)], [('ps', pab)])
        abv = ps[pab][0:64, 0:NCH * 16].rearrange("p (c k) -> p c k", c=NCH)
        G = {n: t[:, 0:NCH, :] for n, t in gs.items()}
        kb.tt('dve', G['xa'], abv[:, :, 0:8], tb[:, 0:8].unsqueeze(1).to_broadcast([64, NCH, 8]), ALU.add,
              [('ps', pab), ('tb',)], [('gs', 'xa')])
        kb.act(G['ea'], G['xa'], AF.Exp, [('gs', 'xa')], [('gs', 'ea')])
        kb.act(G['ea'], G['ea'], AF.Ln, [('gs', 'ea'), ('epsc',)], [('gs', 'ea')], bias=onec[0:64, 0:1])
        kb.tt('dve', G['g'], G['ea'], negA[:].unsqueeze(1).to_broadcast([64, NCH, 8]), ALU.mult,
              [('gs', 'ea'), ('negA',)], [('gs', 'g')])
        kb.act(G['eb'], abv[:, :, 8:16], AF.Exp, [('ps', pab)], [('gs', 'eb')], scale=-1.0)
        kb.ts('dve', G['eb'], G['eb'], 1.0, None, ALU.add, None, [('gs', 'eb')], [('gs', 'eb')])

        def recip(eng_, out, in_, reads, writes):
            def fn(e):
                return e.reciprocal(out=out, in_=in_)
            return P.op(eng_, fn, reads, writes)
        recip('dve', G['beta'], G['eb'], [('gs', 'eb')], [('gs', 'beta')])
        pg = next_ps()
        gflat = gs['g'][:, 0:NCH, :].rearrange("p c h -> p (c h)")
        kb.mm_group(ps[pg][0:64, 0:NCH * 8], [(c32[:, 0, :], gflat)], [('c32',), ('gs', 'g')], [('ps', pg)])
        kb.mm_group(ps[pg][:, 64:64 + NCH * 8], [(onesf[:, :], gflat)], [('ones',), ('gs', 'g')], [('ps', pg)])
        Gps = ps[pg][0:64, 0:NCH * 8].rearrange("p (c h) -> p c h", c=NCH)
        GLps = ps[pg][:, 64:64 + NCH * 8].rearrange("p (c h) -> p c h", c=NCH)
        kb.act(G['Gs'], Gps, AF.Copy, [('ps', pg)], [('gs', 'Gs')])
        kb.act(G['eG'], Gps, AF.Exp, [('ps', pg)], [('gs', 'eG')])
        kb.act(dlast[:, 0:NCH, :], GLps, AF.Exp, [('ps', pg)], [('dlast',)])
        kb.tt('dve', G['dd'], GLps[0:64], G['Gs'], ALU.subtract, [('ps', pg), ('gs', 'Gs')], [('gs', 'dd')])
        kb.act(G['dec'], G['dd'], AF.Exp, [('gs', 'dd')], [('gs', 'dec')])
        kb.tt('dve', G['bG'], G['beta'], G['eG'], ALU.mult, [('gs', 'beta'), ('gs', 'eG')], [('gs', 'bG')])
        for fp in range(8):
            f0 = 2 * fp
            r = fp % 2
            qv = qkT[:, f0:f0 + 2, 0:W]
            kb.tt('pool', sqb[r][:, 0:2 * W].rearrange("p (a t) -> p a t", a=2), qv, qv, ALU.mult,
                  [('qkv', f0), ('qkv', f0 + 1)], [('sqb', r)])
            b2 = next_ps()
            for hh_ in range(2):
                kb.mm_group(ps[b2][:, hh_ * W:(hh_ + 1) * W], [(ones1[:], sqb[r][:, hh_ * W:(hh_ + 1) * W])],
                            [('ones',), ('sqb', r)], [('ps', b2)])
            rsqrt_from_ps(rs32[r][:, 0:2 * W], b2, 2 * W, bias_ap=(lnq[:, 0:1] if f0 < 8 else None), rd=[('rs32', r)])
            kb.tt('dve', qv, qv, rs32[r][:, 0:2 * W].rearrange("p (a t) -> p a t", a=2), ALU.mult,
                  [('qkv', f0), ('qkv', f0 + 1), ('rs32', r)], [('qkv', f0), ('qkv', f0 + 1)])
        for j in range(4):
            b = next_ps()
            for c in range(NCH):
                kb.mm_group(ps[b][0:64, c * 128:(c + 1) * 128], [(vsb[:, j, c * 64:(c + 1) * 64], idb[:, 0, :])],
                            [('vsb', j), ('idb',)], [('ps', b)])
            kb.copy('dve', vtm[:, j, 0:nseg * HV * 128].rearrange("p (s c d) -> p s c d", s=nseg, c=HV)[:, :, 2:2 + cps, :],
                    ps[b][0:64, 0:NCH * 128].rearrange("p (s c d) -> p s c d", s=nseg, c=cps),
                    [('ps', b)], [('vtm', j)])
        if STOP <= 1:
            return
        qk_reads = [('qkv', f) for f in range(24)]
        HGN = 2
        HW_ = 256
        I3 = c32[:, 3, :].unsqueeze(1).to_broadcast([64, 4, 64])

        def v3(ap):
            return ap.rearrange("p (h j) -> p h j", h=4)

        def gdn_gen():
            for c in range(NCH):
                tk = slice(c * 64, (c + 1) * 64)
                seq = seq0 + (c if nseg == 4 else 0)
                if nseg == 4:
                    kb.dma(S32[:], Sin_d[:, c, :, :], [], [('S32', 0), ('S32', 1)])
                    kb.copy('pool', Sbf[0][:], S32[:], [('S32', 0), ('S32', 1)], [('Sbf', 0, 0), ('Sbf', 0, 1)])
                kb.tt('pool', Pc[:], gs['g'][:, c, :].unsqueeze(2).to_broadcast([64, 8, 64]),
                      c32[:, 1, :].unsqueeze(1).to_broadcast([64, 8, 64]), ALU.mult,
                      [('gs', 'g'), ('c32',)], [('hid', 4), ('hid', 5)])
                kb.tt('pool', BS[:], gs['beta'][:, c, :].unsqueeze(2).to_broadcast([64, 8, 64]),
                      c32[:, 1, :].unsqueeze(1).to_broadcast([64, 8, 64]), ALU.mult,
                      [('gs', 'beta'), ('c32',)], [('hid', 8), ('hid', 9)])
                kb.tt('pool', rhsE[:], gs['eG'][:, c, :].unsqueeze(2).to_broadcast([64, 8, 64]),
                      c32[:, 3, :].unsqueeze(1).to_broadcast([64, 8, 64]), ALU.mult,
                      [('gs', 'eG'), ('c32',)], [('hid', 10), ('hid', 11)])
                Pcf = Pc[:].rearrange("p h j -> p (h j)")
                BSf = BS[:].rearrange("p h j -> p (h j)")
                rhsEf = rhsE[:].rearrange("p h j -> p (h j)")
                HS = [slice(hg * HW_, (hg + 1) * HW_) for hg in range(HGN)]
                bD, bkk, bqk = {}, {}, {}
                for hg in range(HGN):
                    bD[hg], bkk[hg], bqk[hg] = next_ps(), next_ps(), next_ps()
                    kb.mm_group(ps[bD[hg]][0:64, 0:HW_], [(c32[:, 0, :], Pcf[:, HS[hg]])],
                                [('c32',), ('hid', 4), ('hid', 5)], [('ps', bD[hg])])
                    for hh in range(4):
                        h = hg * 4 + hh
                        kb.mm_group(ps[bkk[hg]][0:64, hh * 64:(hh + 1) * 64], [(qkT[:, 8 + h, tk], qkT[:, 8 + h, tk])],
                                    qk_reads, [('ps', bkk[hg])])
                    for hh in range(4):
                        h = hg * 4 + hh
                        kb.mm_group(ps[bqk[hg]][0:64, hh * 64:(hh + 1) * 64], [(qkT[:, h, tk], qkT[:, 8 + h, tk])],
                                    qk_reads, [('ps', bqk[hg])])
                for hg in range(HGN):
                    kb.act(Em[:, HS[hg]], ps[bD[hg]][0:64, 0:HW_], AF.Exp, [('ps', bD[hg])], [('Em', hg)])
                for hg in range(HGN):
                    kb.tt('dve', tA[0][:, HS[hg]], ps[bkk[hg]][0:64, 0:HW_], Em[:, HS[hg]], ALU.mult,
                          [('ps', bkk[hg]), ('Em', hg)], [('tA', 0, hg)])
                    kb.tt('dve', Nb[0][:, HS[hg]], tA[0][:, HS[hg]], BSf[:, HS[hg]], ALU.mult,
                          [('tA', 0, hg), ('hid', 8), ('hid', 9)], [('Nb', 0, hg)])
                    kb.tt('pool', v3(INb[:, HS[hg]]), v3(Nb[0][:, HS[hg]]), I3, ALU.add,
                          [('Nb', 0, hg), ('c32',)], [('INb', hg), ('hid', 12)])
                    kb.tt('dve', tA[1][:, HS[hg]], ps[bqk[hg]][0:64, 0:HW_], Em[:, HS[hg]], ALU.mult,
                          [('ps', bqk[hg]), ('Em', hg)], [('tA', 1, hg)])
                    kb.tt('pool', v3(QKb[:, HS[hg]]), v3(tA[1][:, HS[hg]]),
                          c32[:, 2, :].unsqueeze(1).to_broadcast([64, 4, 64]), ALU.mult,
                          [('tA', 1, hg), ('c32',)], [('QKb', hg)])
                yield
                for outs in ('k', 'v'):
                    for hg in range(HGN):
                        b = next_ps()
                        for hh in range(4):
                            h = hg * 4 + hh
                            srcT = qkT[:, 8 + h, tk] if outs == 'k' else vT[:, h, tk]
                            kb.mm_group(ps[b][0:64, hh * 128:(hh + 1) * 128], [(srcT, idb[:, 0, :])],
                                        qk_reads + [('idb',)], [('ps', b)])
                        pvw = ps[b][0:64, :].rearrange("p (h d) -> p h d", h=4)
                        hsl = slice(hg * 4, hg * 4 + 4)
                        if outs == 'v':
                            kb.tt('dve', vbt[0:64, hsl, :], pvw, gs['beta'][:, c, hsl].unsqueeze(2).to_broadcast([64, 4, 128]),
                                  ALU.mult, [('ps', b), ('gs', 'beta')], [('vbt', hg)])
                        else:
                            kb.tt('dve', kbt[:, hsl, :], pvw, gs['bG'][:, c, hsl].unsqueeze(2).to_broadcast([64, 4, 128]),
                                  ALU.mult, [('ps', b), ('gs', 'bG')], [('kbt', hg)])
                            kb.tt('dve', kdt[:, hsl, :], pvw, gs['dec'][:, c, hsl].unsqueeze(2).to_broadcast([64, 4, 128]),
                                  ALU.mult, [('ps', b), ('gs', 'dec')], [('kdt', hg)])
                for hg in range(HGN):
                    bE = next_ps()
                    kb.mm_group(ps[bE][:, 0:HW_], [(onesf[:, :], rhsEf[:, HS[hg]])],
                                [('ones',), ('hid', 10), ('hid', 11)], [('ps', bE)])
                    kb.tt('dve', v3(qdecT[:, HS[hg]]), qkT[:, hg * 4:hg * 4 + 4, tk], v3(ps[bE][:, 0:HW_]), ALU.mult,
                          qk_reads + [('ps', bE)], [('qdecT', hg)])
                yield
                bM, bQ = {}, {}
                for hg in range(HGN):
                    bM[hg], bQ[hg] = next_ps(), next_ps()
                    for hh in range(4):
                        cs = slice(hg * HW_ + hh * 64, hg * HW_ + (hh + 1) * 64)
                        kb.mm_group(ps[bM[hg]][0:64, hh * 64:(hh + 1) * 64], [(Nb[0][:, cs], idb[0:64, 0, 0:64])],
                                    [('Nb', 0, hg), ('idb',)], [('ps', bM[hg])])
                    for hh in range(4):
                        cs = slice(hg * HW_ + hh * 64, hg * HW_ + (hh + 1) * 64)
                        kb.mm_group(ps[bQ[hg]][0:64, hh * 64:(hh + 1) * 64], [(QKb[:, cs], idb[0:64, 0, 0:64])],
                                    [('QKb', hg), ('idb',)], [('ps', bQ[hg])])
                for hg in range(HGN):
                    kb.act(Mb[0][:, HS[hg]], ps[bM[hg]][0:64, 0:HW_], AF.Copy, [('ps', bM[hg])], [('Mb', 0, hg)])
                    kb.tt('dve', v3(Rb[0][0:64, HS[hg]]), I3, v3(ps[bM[hg]][0:64, 0:HW_]), ALU.subtract,
                          [('c32',), ('ps', bM[hg])], [('Rb', 0, hg)])
                    kb.act(QKT[0:64, HS[hg]], ps[bQ[hg]][0:64, 0:HW_], AF.Copy, [('ps', bQ[hg])], [('QKT', hg)])
                yield
                cur = 0
                rc = 0
                for lev in range(1, 7):
                    nxt = 1 - cur
                    bN, bM2, bR = {}, {}, {}
                    for hg in range(HGN):
                        if lev <= 5:
                            bN[hg] = next_ps()
                            for hh in range(4):
                                cs = slice(hg * HW_ + hh * 64, hg * HW_ + (hh + 1) * 64)
                                kb.mm_group(ps[bN[hg]][0:64, hh * 64:(hh + 1) * 64], [(Mb[cur][:, cs], Nb[cur][:, cs])],
                                            [('Mb', cur, hg), ('Nb', cur, hg)], [('ps', bN[hg])])
                        if lev <= 4:
                            bM2[hg] = next_ps()
                            for hh in range(4):
                                cs = slice(hg * HW_ + hh * 64, hg * HW_ + (hh + 1) * 64)
                                kb.mm_group(ps[bM2[hg]][0:64, hh * 64:(hh + 1) * 64], [(Nb[cur][:, cs], Mb[cur][:, cs])],
                                            [('Mb', cur, hg), ('Nb', cur, hg)], [('ps', bM2[hg])])
                        if lev >= 2:
                            bR[hg] = next_ps()
                            for hh in range(4):
                                cs = slice(hg * HW_ + hh * 64, hg * HW_ + (hh + 1) * 64)
                                kb.mm_group(ps[bR[hg]][0:64, hh * 64:(hh + 1) * 64], [(Nb[cur][:, cs], Rb[rc][0:64, cs])],
                                            [('Nb', cur, hg), ('Rb', rc, hg)], [('ps', bR[hg])])
                    for hg in range(HGN):
                        if lev <= 5:
                            kb.act(Nb[nxt][:, HS[hg]], ps[bN[hg]][0:64, 0:HW_], AF.Copy, [('ps', bN[hg])], [('Nb', nxt, hg)])
                        if lev <= 4:
                            kb.act(Mb[nxt][:, HS[hg]], ps[bM2[hg]][0:64, 0:HW_], AF.Copy, [('ps', bM2[hg])], [('Mb', nxt, hg)])
                        if lev >= 2:
                            kb.tt('dve', Rb[1 - rc][0:64, HS[hg]], ps[bR[hg]][0:64, 0:HW_], Rb[rc][0:64, HS[hg]], ALU.add,
                                  [('ps', bR[hg]), ('Rb', rc, hg)], [('Rb', 1 - rc, hg)])
                    if lev >= 2:
                        rc = 1 - rc
                    cur = nxt
                    yield
                Tt = Rb[rc]
                yield
                bT, bF = {}, {}
                for hg in range(HGN):
                    bT[hg], bF[hg] = next_ps(), next_ps()
                    for hh in range(4):
                        cs = slice(hg * HW_ + hh * 64, hg * HW_ + (hh + 1) * 64)
                        kb.mm_group(ps[bT[hg]][0:64, hh * 64:(hh + 1) * 64], [(Tt[0:64, cs], idb[0:64, 0, 0:64])],
                                    [('Rb', rc, hg), ('idb',)], [('ps', bT[hg])])
                    for hh in range(4):
                        cs = slice(hg * HW_ + hh * 64, hg * HW_ + (hh + 1) * 64)
                        kb.mm_group(ps[bF[hg]][0:64, hh * 64:(hh + 1) * 64], [(INb[:, cs], Tt[0:64, cs])],
                                    [('Rb', rc, hg), ('INb', hg), ('hid', 12)], [('ps', bF[hg])])
                for hg in range(HGN):
                    kb.act(Tn[:, HS[hg]], ps[bT[hg]][0:64, 0:HW_], AF.Copy, [('ps', bT[hg])], [('Tn', hg)])
                    kb.tt('dve', v3(Fb[:, HS[hg]]), I3, v3(ps[bF[hg]][0:64, 0:HW_]), ALU.subtract,
                          [('c32',), ('ps', bF[hg])], [('Fb', hg)])
                yield
                bC = {}
                for hg in range(HGN):
                    bC[hg] = next_ps()
                    for hh in range(4):
                        cs = slice(hg * HW_ + hh * 64, hg * HW_ + (hh + 1) * 64)
                        kb.mm_group(ps[bC[hg]][0:64, hh * 64:(hh + 1) * 64], [(Tn[:, cs], Fb[:, cs])],
                                    [('Tn', hg), ('Fb', hg)], [('ps', bC[hg])])
                for hg in range(HGN):
                    kb.act(Cb[0:64, HS[hg]], ps[bC[hg]][0:64, 0:HW_], AF.Copy, [('ps', bC[hg])], [('Cb', hg)])
                yield
                bw = {}
                for hg in range(HGN):
                    bw[hg] = next_ps()
                    for hh in range(4):
                        h = hg * 4 + hh
                        cs = slice(hg * HW_ + hh * 64, hg * HW_ + (hh + 1) * 64)
                        kb.mm_group(ps[bw[hg]][:, hh * 64:(hh + 1) * 64],
                                    [(kbt[:, h, :], Tt[0:64, cs]), (kbt[:, h, :], Cb[0:64, cs])],
                                    [('kbt', hg), ('Rb', rc, hg), ('Cb', hg)], [('ps', bw[hg])])
                for hg in range(HGN):
                    kb.act(negwT[:, HS[hg]], ps[bw[hg]][:, 0:HW_], AF.Copy, [('ps', bw[hg])], [('negwT', hg)], scale=-1.0)
                yield
                sc_ = c % 2 if nseg == 1 else 0
                nsc = (1 - sc_) if nseg == 1 else 0
                Sc = Sbf[sc_]
                bv, bo, bS = {}, {}, {}
                for hg in range(HGN):
                    bv[hg] = next_ps()
                    for hh in range(4):
                        h = hg * 4 + hh
                        cs = slice(hg * HW_ + hh * 64, hg * HW_ + (hh + 1) * 64)
                        kb.mm_group(ps[bv[hg]][0:64, hh * 128:(hh + 1) * 128],
                                    [(Tt[:, cs], vbt[:, h, :]), (Cb[:, cs], vbt[:, h, :]), (negwT[:, cs], Sc[:, h, :])],
                                    [('Rb', rc, hg), ('Cb', hg), ('vbt', hg), ('negwT', hg), ('Sbf', sc_, hg)],
                                    [('ps', bv[hg])])
                for hg in range(HGN):
                    kb.act(vnew[0:64, hg * 4:hg * 4 + 4, :], ps[bv[hg]][0:64, :].rearrange("p (h d) -> p h d", h=4),
                           AF.Copy, [('ps', bv[hg])], [('vnew', hg)])
                yield
                for hg in range(HGN):
                    bS[hg] = next_ps()
                    for hh in range(4):
                        h = hg * 4 + hh
                        kb.mm_group(ps[bS[hg]][:, hh * 128:(hh + 1) * 128], [(kdt[:, h, :], vnew[0:64, h, :])],
                                    [('kdt', hg), ('vnew', hg)], [('ps', bS[hg])])
                    bo[hg] = next_ps()
                    for hh in range(4):
                        h = hg * 4 + hh
                        cs = slice(hg * HW_ + hh * 64, hg * HW_ + (hh + 1) * 64)
                        kb.mm_group(ps[bo[hg]][:, hh * 64:(hh + 1) * 64], [(Sc[:, h, :], qdecT[:, cs]), (vnew[:, h, :], QKT[:, cs])],
                                    [('Sbf', sc_, hg), ('qdecT', hg), ('vnew', hg), ('QKT', hg)], [('ps', bo[hg])])
                for hg in range(HGN):
                    hsl = slice(hg * 4, hg * 4 + 4)
                    kb.tt('pool', S32[:, hsl, :], S32[:, hsl, :], dlast[:, c, hsl].unsqueeze(2).to_broadcast([128, 4, 128]),
                          ALU.mult, [('S32', hg), ('dlast',)], [('S32', hg)])
                    kb.tt('dve', S32[:, hsl, :], S32[:, hsl, :], ps[bS[hg]][:, :].rearrange("p (h d) -> p h d", h=4), ALU.add,
                          [('S32', hg), ('ps', bS[hg])], [('S32', hg)])
                    kb.copy('pool', Sbf[nsc][:, hsl, :], S32[:, hsl, :], [('S32', hg)], [('Sbf', nsc, hg)])
                    kb.act(oTb[:, hsl, tk], ps[bo[hg]][:, 0:HW_].rearrange("p (h j) -> p h j", h=4), AF.Copy,
                           [('ps', bo[hg])], [('tmp',)])
                if nseg == 4 or (last and c == NCH - 1):
                    kb.dma(Sout_d[:, seq, :, :], S32[:], [('S32', 0), ('S32', 1)], [('Sout', seq)])
                yield
        def swa_A(c, j, pr):
            tk = slice(c * 64, (c + 1) * 64)
            sgi = c if nseg == 4 else 0
            lc = 0 if nseg == 4 else c
            gch = gch0 + lc
            rvalid = [r for r in range(3) if (nseg == 4 or gch - 2 + r >= 0)]
            bs_ = [next_ps(), next_ps()]
            for r in rvalid:
                kcol = sgi * HK + (lc + r) * 64
                for g in range(4):
                    par, a = g % 2, g // 2
                    base = par * 64
                    kb.mm_group(ps[bs_[par]][0:64, r * 128 + a * 64:r * 128 + (a + 1) * 64],
                                [(ksT[base:base + 64, j, kcol:kcol + 64], qsT[base:base + 64, 2 * j + a, tk])],
                                [('ksT', j), ('qsT', 2 * j + a)], [('ps', bs_[par])])
            r0, r1 = min(rvalid) * 128, (max(rvalid) + 1) * 128
            for par in range(2):
                kb.act(pT[pr][:, par * 384 + r0:par * 384 + r1], ps[bs_[par]][0:64, r0:r1], AF.Exp,
                       [('ps', bs_[par])], [('pT', pr)])
            return rvalid

        def swa_B(c, j, pr, rvalid):
            tk = slice(c * 64, (c + 1) * 64)
            sgi = c if nseg == 4 else 0
            lc = 0 if nseg == 4 else c
            bo = next_ps()
            for par in range(2):
                kb.mm_group(ps[bo][:, par * 128:(par + 1) * 128],
                            [(vtm[:, j, (sgi * HV + lc + r) * 128:(sgi * HV + lc + r + 1) * 128],
                              pT[pr][:, par * 384 + r * 128:par * 384 + (r + 1) * 128]) for r in rvalid],
                            [('vtm', j), ('pT', pr)], [('ps', bo)])
            for par in range(2):
                kb.mm_group(ps[bo][:, 256 + par * 128:256 + (par + 1) * 128],
                            [(ones1[0:64, :], pT[pr][:, par * 384 + r * 128:par * 384 + (r + 1) * 128]) for r in rvalid],
                            [('ones',), ('pT', pr)], [('ps', bo)])
            kb.tt('dve', rden[pr][:].rearrange("p (g l) -> p g l", g=4),
                  ps[bo][:, 256:512].rearrange("p (g l) -> p g l", g=4),
                  esnk[:, j * 4:(j + 1) * 4].unsqueeze(2).to_broadcast([128, 4, 64]), ALU.add,
                  [('ps', bo), ('esnk',)], [('rden', pr)])
            recip('dve', rden[pr][:], rden[pr][:], [('rden', pr)], [('rden', pr)])
            for par in range(2):
                base = par * 64
                kb.tt('dve', osT[base:base + 64, 2 * j:2 * j + 2, tk],
                      ps[bo][base:base + 64, par * 128:(par + 1) * 128].rearrange("p (a l) -> p a l", a=2),
                      rden[pr][base:base + 64, par * 128:(par + 1) * 128].rearrange("p (a l) -> p a l", a=2), ALU.mult,
                      [('ps', bo), ('rden', pr)], [('osT', 2 * j), ('osT', 2 * j + 1)])

        def swa_gen():
            units = [(c, j) for c in range(NCH) for j in range(4)]
            pend = None
            for u, (c, j) in enumerate(units):
                rv = swa_A(c, j, u % 2)
                yield
                if pend is not None:
                    swa_B(*pend)
                    yield
                pend = (c, j, u % 2, rv)
            swa_B(*pend)
            yield

        gens = [gdn_gen()] + ([swa_gen()] if STOP > 3 else [])
        while gens:
            for g_ in list(gens):
                try:
                    next(g_)
                except StopIteration:
                    gens.remove(g_)
        if STOP <= 2:
            return
        for hp in range(4):
            h0 = 2 * hp
            r = hp % 2
            ov = oTb[:, h0:h0 + 2, 0:W]
            kb.act(sqb[r][:, 0:2 * W].rearrange("p (a t) -> p a t", a=2), ov, AF.Square, [('tmp',)], [('sqb', r)])
            b2 = next_ps()
            for hh_ in range(2):
                kb.mm_group(ps[b2][:, hh_ * W:(hh_ + 1) * W], [(ones128[:], sqb[r][:, hh_ * W:(hh_ + 1) * W])],
                            [('ones',), ('sqb', r)], [('ps', b2)])
            rsqrt_from_ps(rs32[r][:, 0:2 * W], b2, 2 * W, rd=[('rs32', r)])
            kb.tt('dve', ov, ov, rs32[r][:, 0:2 * W].rearrange("p (a t) -> p a t", a=2), ALU.mult,
                  [('tmp',), ('rs32', r)], [('tmp',)])
            kb.stt('dve', ov, ov, pp[:, PP_GN:PP_GN + 1], zT[:, h0:h0 + 2, 0:W], ALU.mult, ALU.mult,
                   [('tmp',), ('pp',), ('zT', h0), ('zT', h0 + 1)], [('tmp',)])
        if STOP <= 4:
            return
        if nseg == 1 and not last:
            for j in range(4):
                kb.copy('pool', ksT[:, j, 0:128], ksT[:, j, L:L + 128], [('ksT', j)], [('ksT', j)])
                kb.copy('pool', vtm[:, j, 0:256], vtm[:, j, cps * 128:(cps + 2) * 128], [('vtm', j)], [('vtm', j)])
        for f in range(8):
            kb.tt('dve', mg[:, 0:W], oTb[:, f, 0:W], gates[:, f, 0:W], ALU.mult, [('tmp',), ('gates', f)], [('mg',)])
            kb.tt('dve', osT[:, f, 0:W], osT[:, f, 0:W], gates[:, 8 + f, 0:W], ALU.mult,
                  [('osT', f), ('gates', 8 + f)], [('osT', f)])
            kb.tt('dve', hT[:, f, 0:W], mg[:, 0:W], osT[:, f, 0:W], ALU.add, [('mg',), ('osT', f)], [('hT', f)])
        for g in range(2):
            s, w = load_w('wout', g, KC, 512)
            for jj in range(4):
                oc = 4 * g + jj
                b = next_ps()
                kb.mm_group(ps[b][:, 0:W], [(w[:, kc, jj * 128:(jj + 1) * 128], hT[:, kc, 0:W]) for kc in range(KC)],
                            [('wbuf', s)] + [('hT', kc) for kc in range(KC)], [('ps', b)])
                for (a, b_, sq_) in segs:
                    kb.stt('dve', x_sb[:, oc, m0 + a:m0 + b_], ps[b][:, a:b_], Gmod[:, 1, oc, sq_:sq_ + 1],
                           x_sb[:, oc, m0 + a:m0 + b_], ALU.mult, ALU.add,
                           [('ps', b), ('Gmod',), ('x',)], [('x',)])

    xTv = xT.rearrange("(k p) t -> p k t", p=128)
    yTv = yT.rearrange("(k p) t -> p k t", p=128)
    for (t0, TW, segs, msubs) in tiles:
        kb.dma(x_sb[:, :, 0:TW], xTv[:, :, t0:t0 + TW], [], [('x',)])
        ffn(0, 'f1i', 'f1o', TW, segs)
        if do_mixer:
            if len(msubs) == 2 and cfg.get('hoist', True):
                ms0, ms1 = msubs
                (m1, ns1, L1, sq1) = ms1[0:4]
                segs1 = [(j * L1, (j + 1) * L1, sq1 + j) for j in range(ns1)]
                mixer(*ms0, after_proj=lambda: norm_mod(1, m1, ns1 * L1, segs1, alt=True))
                mixer(*ms1, prenormed=True)
            else:
                for ms in msubs:
                    mixer(*ms)
        ffn(2, 'f2i', 'f2o', TW, segs)
        kb.dma(yTv[:, :, t0:t0 + TW], x_sb[:, :, 0:TW], [('x',)], [('yT', t0)])
    kb.dma(histout_d[:, :, :, :], hist[:], [('hist',)], [('histout',)])

    P.emit(kb.es)
    kb.es.close()
    return nc


def make_tiles(n_prompt_tiles=4, sample=True, seqs=(0, 1)):
    tiles = []
    for s in seqs:
        for i in range(n_prompt_tiles):
            msubs = []
            for hh in range(2):
                msubs.append((hh * 256, 1, 256, s, (i * 2 + hh) * 4, (i == 0 and hh == 0),
                              (i == TP // 512 - 1 and hh == 1)))
            tiles.append((s * TP + i * 512, 512, [(0, 512, s)], msubs))
    if sample:
        tiles.append((2 * TP, 4 * TS, [(j * TS, (j + 1) * TS, 2 + j) for j in range(4)],
                      [(0, 4, 64, 2, 0, True, True)]))
    return tiles


def host_win_cols():
    cols = []
    for ch in range(24):
        cols.append(np.arange(ch * 128, (ch + 1) * 128))
    for h in range(8):
        cols.append(3072 + np.arange(h * 128, (h + 1) * 128))
    for i in range(8):
        cols.append(4112 + np.arange(i * 128, (i + 1) * 128))
    for j in range(4):
        c = 5136 + np.arange(j * 64, (j + 1) * 64)
        cols.append(np.concatenate([c, c]))
    for j in range(4):
        c = 5392 + np.arange(j * 64, (j + 1) * 64)
        cols.append(np.concatenate([c, c]))
    for f in range(16):
        cols.append(5648 + np.arange(f * 128, (f + 1) * 128))
    return cols


def host_inputs(inp, core, cache):
    f = np.float32
    m = {}
    xp = inp['x_prompt'][2 * core:2 * core + 2].reshape(2 * TP, D)
    xs = inp['x_sample'][4 * core:4 * core + 4].reshape(4 * TS, D)
    m['xT'] = np.ascontiguousarray(np.concatenate([xp, xs], 0).T.astype(f))
    c = np.concatenate([inp['c_prompt'][2 * core:2 * core + 2], inp['c_sample'][4 * core:4 * core + 4]], 0)
    m['cT'] = np.ascontiguousarray(c.T.reshape(KC, 128, NSEQ).transpose(1, 0, 2).astype(f))
    sl = slice(4 * core, 4 * core + 4)
    S = inp['state_gdn'][0, sl]
    m['Sin'] = np.ascontiguousarray(S.transpose(2, 0, 1, 3).astype(f))
    cs = inp['state_gdn_conv'][0, sl]
    m['histin'] = np.ascontiguousarray(cs.reshape(4, 3, 24, 128).transpose(3, 2, 0, 1).astype(f))
    kc_ = inp['cache_swa_k'][0, sl]
    vc_ = inp['cache_swa_v'][0, sl]
    kfm = kc_.transpose(3, 0, 2, 1)
    m['kcache'] = np.ascontiguousarray(np.concatenate([kfm, kfm], 0).astype(f))
    vfm = vc_.transpose(3, 0, 2, 1)
    m['vcache'] = np.ascontiguousarray(np.concatenate([vfm, vfm], 0).astype(f))
    vt = vc_.reshape(4, 2, 64, 4, 64).transpose(2, 3, 0, 1, 4)
    m['vcachetm'] = np.ascontiguousarray(np.concatenate([vt, vt], -1).astype(f))
    if 'shared' not in cache:
        sh = {}
        sh['w_ada'] = np.ascontiguousarray(inp['w_ada'][0].reshape(KC, 128, NMOD * D).transpose(1, 0, 2))
        sh['b_ada'] = np.ascontiguousarray(inp['b_ada'][0].reshape(NMOD * KC, 128).T)
        g = np.stack([inp['norm_ffn1'][0], inp['norm_mix'][0], inp['norm_ffn2'][0]], 0)
        sh['gains'] = np.ascontiguousarray(g.reshape(3, KC, 128).transpose(2, 0, 1))
        sh['f1i'] = host_group_weights(inp['ffn1_w_in'][0], ffn_in_groups(), KC)
        sh['f1o'] = host_group_weights(inp['ffn1_w_out'][0], ffn_out_groups(), FC)
        sh['f2i'] = host_group_weights(inp['ffn2_w_in'][0], ffn_in_groups(), KC)
        sh['f2o'] = host_group_weights(inp['ffn2_w_out'][0], ffn_out_groups(), FC)
        wc = host_win_cols()
        sh['win'] = host_group_weights(inp['w_in'][0], [np.concatenate(wc[4 * g:4 * g + 4]) for g in range(16)], KC)
        sh['wout'] = host_group_weights(inp['w_out'][0], [np.arange(g * 512, (g + 1) * 512) for g in range(2)], KC)
        sh['wab'] = np.ascontiguousarray(inp['w_in'][0][:, 4096:4112].reshape(KC, 128, 16).transpose(1, 0, 2))
        i_ = np.arange(64)
        c32 = np.zeros((64, 5, 64), f)
        c32[:, 0, :] = (i_[:, None] <= i_[None, :])
        c32[:, 1, :] = (i_[:, None] > i_[None, :])
        c32[:, 2, :] = (i_[None, :] <= i_[:, None])
        c32[:, 3, :] = np.eye(64)
        c32[:, 4, :] = 1.0
        sh['c32'] = c32
        id128 = np.zeros((128, 2, 128), f)
        id128[:, 0, :] = np.eye(128)
        id128[0:64, 1, 0:64] = 1.0 / 64
        id128[64:128, 1, 64:128] = 1.0 / 64
        sh['id128'] = id128
        pp = np.zeros((128, NPP), f)
        cw = inp['gdn_conv_w'][0]
        pp[:, PP_CW:PP_CW + 96] = cw.reshape(4, 24, 128).transpose(2, 1, 0).reshape(128, 96)
        pp[:, PP_BM:PP_BM + 16] = inp['b_merge'][0].reshape(16, 128).T
        pp[:, PP_GN] = inp['gdn_norm'][0]
        pp[:, PP_QG] = np.tile(inp['swa_q_norm'][0], 2)
        pp[:, PP_KG] = np.tile(inp['swa_k_norm'][0], 2)
        sh['pp'] = pp
        tb = np.zeros((64, 16), f)
        tb[:, 0:8] = inp['gdn_dt_bias'][0][None, :]
        tb[:, 8:16] = inp['gdn_a_log'][0][None, :]
        sh['tb'] = tb
        sk = inp['swa_sinks'][0].reshape(4, 2, 2).transpose(0, 2, 1).reshape(16)
        sh['snk'] = np.ascontiguousarray(np.broadcast_to(sk[None, :], (128, 16)).astype(f))
        cache['shared'] = sh
    m.update(cache['shared'])
    return m


def run(inputs, cfg):
    inputs = {k: np.asarray(v) for k, v in inputs.items()}
    nc = build(cfg)
    cache = {}
    in_maps = [host_inputs(inputs, c, cache) for c in range(NCORES)]
    if not cfg.get('mixer', True):
        keep = ('xT', 'cT', 'w_ada', 'b_ada', 'gains', 'f1i', 'f1o', 'f2i', 'f2o')
    res = run_bass_kernel_spmd(nc, in_maps, core_ids=list(range(NCORES)))
    return res.results


def assemble(results):
    B, Bs = 16, 32
    f = np.float32
    yp = np.zeros((B, TP, D), f)
    ys = np.zeros((Bs, TS, D), f)
    conv_p = np.zeros((1, B, 3, 3072), f)
    conv_s = np.zeros((1, Bs, 3, 3072), f)
    S_p = np.zeros((1, B, 8, 128, 128), f)
    S_s = np.zeros((1, Bs, 8, 128, 128), f)
    k_p = np.zeros((1, B, 128, 4, 64), f)
    v_p = np.zeros((1, B, 128, 4, 64), f)
    k_s = np.zeros((1, Bs, 128, 4, 64), f)
    v_s = np.zeros((1, Bs, 128, 4, 64), f)
    for c in range(NCORES):
        r = results[c]
        y = r['yT'].T
        yp[2 * c:2 * c + 2] = y[:2 * TP].reshape(2, TP, D)
        ys[4 * c:4 * c + 4] = y[2 * TP:].reshape(4, TS, D)
        h = r['histout']
        hh = h.transpose(2, 3, 1, 0).reshape(NSEQ, 3, 3072)
        conv_p[0, 2 * c:2 * c + 2] = hh[0:2]
        conv_s[0, 4 * c:4 * c + 4] = hh[2:6]
        S = r['Sout'].transpose(1, 2, 0, 3)
        S_p[0, 2 * c:2 * c + 2] = S[0:2]
        S_s[0, 4 * c:4 * c + 4] = S[2:6]
        ko = r['kout'][0:64].transpose(1, 3, 2, 0)
        vo = r['vout'][0:64].transpose(1, 3, 2, 0)
        k_p[0, 2 * c:2 * c + 2] = ko[0:2]
        k_s[0, 4 * c:4 * c + 4] = ko[2:6]
        v_p[0, 2 * c:2 * c + 2] = vo[0:2]
        v_s[0, 4 * c:4 * c + 4] = vo[2:6]
    return (yp, ys, conv_p, S_p, k_p, v_p, conv_s, S_s, k_s, v_s)


def kernel(**inputs):
    cfg = {'tiles': make_tiles()}
    results = run(inputs, cfg)
    return assemble(results)
```

```python
import numpy as np
from contextlib import ExitStack
import concourse.bass as bass
import concourse.mybir as mybir
from concourse.bass_utils import run_bass_kernel_spmd

F32 = mybir.dt.float32
BF16 = mybir.dt.bfloat16
AF = mybir.ActivationFunctionType
ALU = mybir.AluOpType

NCORES = 8
D = 1024
KC = 8
DFF = 2816
FC = 22
NMOD = 9
NSEQ = 6
TP = 2048
TS = 64
NTOK = 2 * TP + 4 * TS
EPS = 1e-6

ENGS = ['pe', 'act', 'dve', 'pool', 'sp']
DMA_ENGS = ('sp',)


class Op:
    __slots__ = ('eng', 'fn', 'deps', 'needed', 'count', 'dma', 'slot', 'val', 'idx', 'vc')


class Prog:
    def __init__(self, nc, n_dma_slots=16):
        self.nc = nc
        self.ops = {e: [] for e in ENGS}
        self.lastw = {}
        self.readers = {}
        self.K = n_dma_slots

    def op(self, eng, fn, reads=(), writes=()):
        o = Op()
        o.eng = eng
        o.fn = fn
        o.idx = len(self.ops[eng])
        o.needed = False
        o.dma = eng in DMA_ENGS or getattr(fn, 'is_dma', False)
        o.slot = None
        deps = set()
        for r in reads:
            w = self.lastw.get(r)
            if w is not None:
                deps.add(w)
            if r[0] == 'ps':
                for rd in self.readers.get(r, ()):
                    if rd[0] != eng:
                        deps.add(rd)
        for wr in writes:
            w = self.lastw.get(wr)
            if w is not None:
                deps.add(w)
            for rd in self.readers.get(wr, ()):
                deps.add(rd)
        if eng == 'pe':
            deps = {d for d in deps if d[0] != 'pe'}
        deps.discard((eng, o.idx))
        prev = self.ops[eng][-1] if self.ops[eng] else None
        vc = dict(prev.vc) if prev is not None else {}
        infos = []
        for d in deps:
            dop = self.ops[d[0]][d[1]]
            after = dict(dop.vc)
            if not dop.dma:
                if after.get(d[0], -1) < d[1]:
                    after[d[0]] = d[1]
            infos.append((d, dop, after))
        keep = set()
        for (d, dop, after) in infos:
            if dop.dma:
                keep.add(d)
                continue
            f, i = d
            implied = vc.get(f, -1) >= i
            if not implied:
                for (d2, dop2, after2) in infos:
                    if d2 != d and after2.get(f, -1) >= i:
                        implied = True
                        break
            if not implied:
                keep.add(d)
        for (d, dop, after) in infos:
            for f, i in after.items():
                if vc.get(f, -1) < i:
                    vc[f] = i
        o.vc = vc
        self.n_pruned = getattr(self, 'n_pruned', 0) + (len(deps) - len(keep))
        self.n_kept = getattr(self, 'n_kept', 0) + len(keep)
        deps = keep
        o.deps = deps
        me = (eng, o.idx)
        for r in reads:
            self.readers.setdefault(r, []).append(me)
        for wr in writes:
            self.lastw[wr] = me
            self.readers[wr] = []
        self.ops[eng].append(o)
        return o

    def finalize(self):
        print("ops:", {e: len(v) for e, v in self.ops.items()}, "deps kept", self.n_kept, "pruned", self.n_pruned)
        for e in ENGS:
            for o in self.ops[e]:
                for (f, i) in o.deps:
                    self.ops[f][i].needed = True
        for e in ENGS:
            k = 0
            c = 0
            for o in self.ops[e]:
                if o.dma:
                    o.slot = k % self.K
                    o.val = 16 * (k // self.K + 1)
                    k += 1
                else:
                    if o.needed:
                        c += 1
                o.count = c

    def emit_engine(self, eng, e, sems, dsems):
        waited = {}

        def wait(key, sem, val):
            if waited.get(key, 0) >= val:
                return
            waited[key] = val
            e.wait_ge(sem, val)

        last = {}
        for o in self.ops[eng]:
            for (f, i) in sorted(o.deps):
                d = self.ops[f][i]
                if d.dma:
                    wait((f, 'd', d.slot), dsems[f][d.slot], d.val)
                else:
                    wait(f, sems[f], d.count)
            if o.dma:
                if o.val > 16:
                    wait((eng, 'd', o.slot), dsems[eng][o.slot], o.val - 16)
                ins = o.fn(e)
                ins.then_inc(dsems[eng][o.slot], 16)
                last[o.slot] = o.val
            else:
                ins = o.fn(e)
                if o.needed:
                    ins.then_inc(sems[eng], 1)
        for s, v in last.items():
            wait((eng, 'd', s), dsems[eng][s], v)

    def emit(self, es):
        nc = self.nc
        self.finalize()
        sems = {e: es.enter_context(nc.semaphore("sem_" + e)) for e in ENGS if e not in DMA_ENGS}
        dsems = {e: [es.enter_context(nc.semaphore("dsem_%s_%d" % (e, k))) for k in range(self.K)]
                 for e in ('sp', 'pool')}
        block = es.enter_context(nc.Block())

        @block.tensor
        def _(e):
            self.emit_engine('pe', e, sems, dsems)

        @block.scalar
        def _(e):
            self.emit_engine('act', e, sems, dsems)

        @block.vector
        def _(e):
            self.emit_engine('dve', e, sems, dsems)

        @block.gpsimd
        def _(e):
            self.emit_engine('pool', e, sems, dsems)

        @block.sync
        def _(e):
            self.emit_engine('sp', e, sems, dsems)


class K:
    def __init__(self, cfg):
        self.cfg = cfg
        nc = self.nc = bass.Bass("TRN2", target_bir_lowering=False)
        self.es = ExitStack()
        self.P = Prog(nc)
        self.dram = {}
        self.sb = {}
        self._psrr = 0

    def din(self, name, shape, dt=F32):
        t = self.nc.dram_tensor(name, list(shape), dt, kind="ExternalInput").ap()
        self.dram[name] = t
        return t

    def dout(self, name, shape, dt=F32):
        t = self.nc.dram_tensor(name, list(shape), dt, kind="ExternalOutput").ap()
        self.dram[name] = t
        return t

    def dscr(self, name, shape, dt):
        t = self.nc.dram_tensor(name, list(shape), dt).ap()
        self.dram[name] = t
        return t

    def tile(self, name, shape, dt):
        t = self.es.enter_context(self.nc.sbuf_tensor("sb_" + name, list(shape), dt))
        self.sb[name] = t
        return t

    def psum(self, name, shape, dt=F32):
        return self.es.enter_context(self.nc.psum_tensor(name, list(shape), dt))

    def dma(self, out, in_, reads, writes, eng='sp'):
        def fn(e):
            return e.dma_start(out=out, in_=in_)
        fn.is_dma = True
        return self.P.op(eng, fn, reads, writes)

    def mm_group(self, out, pairs, reads, writes):
        n = len(pairs)

        def fn(e):
            ins = None
            for i, (l, r) in enumerate(pairs):
                ins = e.matmul(out, l, r, start=(i == 0), stop=(i == n - 1))
            return ins
        return self.P.op('pe', fn, reads, writes)

    def act(self, out, in_, func, reads, writes, bias=None, scale=None):
        def fn(e):
            kw = {}
            if bias is not None:
                kw['bias'] = bias
            if scale is not None:
                kw['scale'] = scale
            return e.activation(out=out, in_=in_, func=func, **kw)
        return self.P.op('act', fn, reads, writes)

    def tt(self, eng, out, in0, in1, op, reads, writes):
        def fn(e):
            return e.tensor_tensor(out=out, in0=in0, in1=in1, op=op)
        return self.P.op(eng, fn, reads, writes)

    def ts(self, eng, out, in0, s1, s2, op0, op1, reads, writes):
        def fn(e):
            if op1 is None:
                return e.tensor_scalar(out=out, in0=in0, scalar1=s1, scalar2=None, op0=op0)
            return e.tensor_scalar(out=out, in0=in0, scalar1=s1, scalar2=s2, op0=op0, op1=op1)
        return self.P.op(eng, fn, reads, writes)

    def stt(self, eng, out, in0, scalar, in1, op0, op1, reads, writes):
        def fn(e):
            return e.scalar_tensor_tensor(out=out, in0=in0, scalar=scalar, in1=in1, op0=op0, op1=op1)
        return self.P.op(eng, fn, reads, writes)

    def copy(self, eng, out, in_, reads, writes):
        def fn(e):
            return e.tensor_copy(out=out, in_=in_)
        return self.P.op(eng, fn, reads, writes)

    def memset(self, eng, ap, val, writes):
        def fn(e):
            return e.memset(ap, val)
        return self.P.op(eng, fn, (), writes)


def host_group_weights(w, col_groups, kc):
    out = []
    for cols in col_groups:
        blk = w[:, cols]
        blk = blk.reshape(kc, 128, len(cols)).transpose(1, 0, 2)
        out.append(blk)
    return np.ascontiguousarray(np.stack(out, 0))


def ffn_in_groups():
    gs = []
    for g in range(FC // 2):
        cols = np.concatenate([np.arange(2 * g * 128, (2 * g + 2) * 128),
                               DFF + np.arange(2 * g * 128, (2 * g + 2) * 128)])
        gs.append(cols)
    return gs


def ffn_out_groups():
    return [np.arange(oc * 128, (oc + 1) * 128) for oc in range(KC)]


NPROJ = 64
PP_CW, PP_BM, PP_GN, PP_QG, PP_KG, NPP = 0, 96, 112, 113, 114, 115


def build(cfg):
    kb = K(cfg)
    nc, P = kb.nc, kb.P
    tiles = cfg['tiles']
    do_mixer = cfg.get('mixer', True)
    STOP = cfg.get('stop', 99)

    xT = kb.din("xT", [D, NTOK])
    yT = kb.dout("yT", [D, NTOK])
    cT = kb.din("cT", [128, KC, NSEQ])
    w_ada = kb.din("w_ada", [128, KC, NMOD * D])
    b_ada = kb.din("b_ada", [128, NMOD * KC])
    gains = kb.din("gains", [128, 3, KC])
    wsrc = {
        'f1i': kb.din("f1i", [FC // 2, 128, KC, 512]),
        'f1o': kb.din("f1o", [KC, 128, FC, 128]),
        'f2i': kb.din("f2i", [FC // 2, 128, KC, 512]),
        'f2o': kb.din("f2o", [KC, 128, FC, 128]),
        'win': kb.din("win", [NPROJ // 4, 128, KC, 512]),
        'wout': kb.din("wout", [2, 128, KC, 512]),
    }
    wscr = {k: kb.dscr(k + "_bf", list(v.shape), BF16) for k, v in wsrc.items()}
    wab_d = kb.din("wab", [128, KC, 16])
    c32_d = kb.din("c32", [64, 5, 64])
    id128_d = kb.din("id128", [128, 2, 128])
    pp_d = kb.din("pp", [128, NPP])
    tb_d = kb.din("tb", [64, 16])
    snk_d = kb.din("snk", [128, 16])
    Sin_d = kb.din("Sin", [128, 4, 8, 128])
    histin_d = kb.din("histin", [128, 24, 4, 3])
    kcache_d = kb.din("kcache", [128, 4, 4, 128])
    vcache_d = kb.din("vcache", [128, 4, 4, 128])
    vcachetm_d = kb.din("vcachetm", [64, 4, 4, 2, 128])
    Sout_d = kb.dout("Sout", [128, NSEQ, 8, 128])
    histout_d = kb.dout("histout", [128, 24, NSEQ, 3])
    kout_d = kb.dout("kout", [128, NSEQ, 4, 128])
    vout_d = kb.dout("vout", [128, NSEQ, 4, 128])

    ones = kb.tile("ones", [128, 128], BF16)
    ones1 = kb.tile("ones1", [128, 128], BF16)
    ones128 = kb.tile("ones128", [128, 128], BF16)
    onesf = kb.tile("onesf", [64, 128], F32)
    epsc = kb.tile("epsc", [128, 1], F32)
    lnq = kb.tile("lnq", [128, 1], F32)
    onec = kb.tile("onec", [128, 1], F32)
    x_sb = kb.tile("x_sb", [128, KC, 512], F32)
    sq = kb.tile("sq", [128, KC, 512], BF16)
    tmp = kb.tile("tmp", [128, KC, 512], BF16)
    rstd = kb.tile("rstd", [128, 512], F32)
    hT = kb.tile("hT", [128, KC, 512], BF16)
    hid = kb.tile("hid", [128, FC, 512], BF16)
    sg = [kb.tile("sg%d" % i, [128, 512], F32) for i in range(2)]
    NWB = 3
    wbuf = [kb.tile("wbuf%d" % i, [128, 4096], BF16) for i in range(NWB)]
    arena = kb.tile("arena", [128, 8192], F32)
    stg32 = [arena[:, 0:4096]]
    stg16 = [arena[:, 4096:6144].bitcast(BF16)]
    wada_sb = [arena[:, 0:4096].rearrange("p (k n) -> p k n", k=KC),
               arena[:, 4096:8192].rearrange("p (k n) -> p k n", k=KC)]
    ab16 = arena[:, :].bitcast(BF16)
    TWm = 256
    qkT = ab16[:, 0:4096].rearrange("p (f t) -> p f t", f=16)
    vT = ab16[:, 4096:6144].rearrange("p (f t) -> p f t", f=8)
    zT = ab16[:, 6144:8192].rearrange("p (f t) -> p f t", f=8)
    qsT = ab16[:, 8192:10240].rearrange("p (f t) -> p f t", f=8)
    gates = ab16[:, 10240:14336].rearrange("p (f t) -> p f t", f=16)
    osT = ab16[:, 14336:16384].rearrange("p (f t) -> p f t", f=8)
    cT_sb = kb.tile("cT_sb", [128, KC, NSEQ], F32)
    scT = kb.tile("scT", [128, KC, NSEQ], F32)
    scTb = kb.tile("scTb", [128, KC, NSEQ], BF16)
    bada_sb = kb.tile("bada_sb", [128, NMOD * KC], F32)
    gains_sb = kb.tile("gains_sb", [128, 3, KC], F32)
    modT = kb.tile("modT", [128, NMOD * KC, NSEQ], F32)
    Amod = kb.tile("Amod", [128, 3, KC, NSEQ], F32)
    Gmod = kb.tile("Gmod", [128, 3, KC, NSEQ], F32)
    wab32 = kb.tile("wab32", [128, KC, 16], F32)
    wab = kb.tile("wabb", [128, KC, 16], BF16)
    c32 = kb.tile("c32", [64, 5, 64], F32)
    id32 = kb.tile("id32", [128, 2, 128], F32)
    idb = kb.tile("idb", [128, 2, 128], BF16)
    pp = kb.tile("pp", [128, NPP], F32)
    tb = kb.tile("tb", [64, 16], F32)
    negA = kb.tile("negA", [64, 8], F32)
    snk = kb.tile("snk", [128, 16], F32)
    esnk = kb.tile("esnk", [128, 16], F32)
    qg8 = kb.tile("qg8", [128, 1], F32)
    hist = kb.tile("hist", [128, 24, NSEQ, 3], F32)
    rawb = [kb.tile("rawb%d" % i, [128, 4 * 67], F32) for i in range(2)]
    cacc = [kb.tile("cacc%d" % i, [128, 256], F32) for i in range(2)]
    sqb = [kb.tile("sqb%d" % i, [128, 512], BF16) for i in range(2)]
    kn32 = [kb.tile("kn32_%d" % i, [128, 512], F32) for i in range(2)]
    rs32 = [kb.tile("rs32_%d" % i, [128, 512], F32) for i in range(2)]
    ksT = kb.tile("ksT", [128, 4, 4 * (128 + 64)], BF16)
    vsb = kb.tile("vsb", [128, 4, 256], BF16)
    vtm = kb.tile("vtm", [64, 4, 4 * 3 * 128], BF16)
    S32 = kb.tile("S32", [128, 8, 128], F32)
    Sbf = [kb.tile("Sbf%d" % i, [128, 8, 128], BF16) for i in range(2)]
    oTb = tmp[:].rearrange("p k t -> p (k t)").bitcast(F32).rearrange("p (h t) -> p h t", h=8)

    def hidf32(k0, nk):
        return hid[:, k0:k0 + nk, :].rearrange("p k t -> p (k t)").bitcast(F32)
    kvst = hidf32(0, 4)
    INb = hid[0:64, 12, :]
    Tn = hid[0:64, 13, :]
    Fb = hid[0:64, 14, :]
    Cb = hid[:, 15, :]
    gs = {n: kb.tile("gs_" + n, [64, 4, 8], F32) for n in ('xa', 'ea', 'g', 'eb', 'beta', 'Gs', 'eG', 'bG', 'dec', 'dd')}
    dlast = kb.tile("dlast", [128, 4, 8], F32)
    Pc = hidf32(4, 2)[0:64, :].rearrange("p (h j) -> p h j", h=8)
    Em = hidf32(6, 2)[0:64, :]
    tA = [kb.tile("tA%d" % i, [64, 512], F32) for i in range(2)]
    BS = hidf32(8, 2)[0:64, :].rearrange("p (h j) -> p h j", h=8)
    Nb = [kb.tile("Nb%d" % i, [64, 512], BF16) for i in range(2)]
    Mb = [kb.tile("Mb%d" % i, [64, 512], BF16) for i in range(2)]
    Rb = [kb.tile("Rb%d" % i, [128, 512], BF16) for i in range(2)]
    QKb = kb.tile("QKb", [64, 512], BF16)
    QKT = kb.tile("QKT", [128, 512], BF16)
    vbt = kb.tile("vbt", [128, 8, 128], BF16)
    kbt = kb.tile("kbt", [64, 8, 128], BF16)
    kdt = kb.tile("kdt", [64, 8, 128], BF16)
    negwT = kb.tile("negwT", [128, 512], BF16)
    qdecT = kb.tile("qdecT", [128, 512], BF16)
    rhsE = hidf32(10, 2)[0:64, :].rearrange("p (h j) -> p h j", h=8)
    vnew = kb.tile("vnew", [128, 8, 128], BF16)
    pT = [kb.tile("pT%d" % i, [64, 768], BF16) for i in range(2)]
    rden = [kb.tile("rden%d" % i, [128, 256], F32) for i in range(2)]
    mg = kb.tile("mg", [128, 256], F32)

    print('SBUF bytes remaining per partition:', nc.sbuf_bytes_remaining)
    NPS = 8
    ps = [kb.psum("ps%d" % i, [128, 512]) for i in range(NPS)]
    rr = {'i': 0}

    def next_ps():
        b = rr['i'] % NPS
        rr['i'] += 1
        key = ('ps', b)
        if key in P.lastw and not P.readers.get(key):
            raise RuntimeError("PSUM bank %d re-allocated while its last result is unconsumed" % b)
        return b

    AR = ('arena',)

    kb.memset('pool', ones[:], 1.0 / 1024.0, [('ones',)])
    kb.memset('pool', ones1[:], 1.0, [('ones',)])
    kb.memset('pool', ones128[:], 1.0 / 128.0, [('ones',)])
    kb.memset('pool', onesf[:], 1.0, [('ones',)])
    kb.memset('pool', epsc[:], EPS, [('epsc',)])
    kb.memset('pool', lnq[:], float(np.log(128.0 ** -0.5)), [('epsc',)])
    kb.memset('pool', onec[:], 1.0, [('epsc',)])
    kb.memset('pool', hist[:], 0.0, [('hist',)])
    kb.memset('pool', Rb[0][:], 0.0, [('Rb', 0, 0), ('Rb', 0, 1)])
    kb.memset('pool', Rb[1][:], 0.0, [('Rb', 1, 0), ('Rb', 1, 1)])
    kb.memset('pool', QKT[:], 0.0, [('QKT', 0), ('QKT', 1)])
    kb.memset('pool', vbt[:], 0.0, [('vbt', 0), ('vbt', 1)])
    kb.memset('pool', vnew[:], 0.0, [('vnew', 0), ('vnew', 1)])
    worder = ([('f1i', g) for g in range(FC // 2)] + [('f1o', g) for g in range(KC)]
              + ([('win', g) for g in range(NPROJ // 4)] + [('wout', g) for g in range(2)] if do_mixer else [])
              + [('f2i', g) for g in range(FC // 2)] + [('f2o', g) for g in range(KC)])
    wpos = {k: i for i, k in enumerate(worder)}
    cast_done = {'n': 0}
    LOOK = 10

    def ensure_cast(upto):
        while cast_done['n'] < min(upto + 1, len(worder)):
            name, g = worder[cast_done['n']]
            cast_done['n'] += 1
            kb.dma(wscr[name][g].rearrange("p k n -> p (k n)"), wsrc[name][g].rearrange("p k n -> p (k n)"),
                   [], [('wscr', name, g)], eng='pool')

    ensure_cast(FC // 2 + KC - 1)
    kb.dma(cT_sb[:], cT[:, :, :], [], [('cT',)])
    kb.dma(bada_sb[:], b_ada[:, :], [], [('bada',)])
    kb.dma(gains_sb[:], gains[:, :, :], [], [('gains',)])
    kb.dma(wab32[:], wab_d[:, :, :], [], [('wab32',)])
    kb.dma(c32[:], c32_d[:, :, :], [], [('c32',)])
    kb.dma(id32[:], id128_d[:, :, :], [], [('id32',)])
    kb.dma(pp[:], pp_d[:, :], [], [('pp',)])
    kb.dma(tb[:], tb_d[:, :], [], [('tb',)])
    kb.dma(snk[:], snk_d[:, :], [], [('snk',)])
    kb.dma(hist[:, :, 2:6, :], histin_d[:, :, :, :], [('hist',)], [('hist',)])
    kb.copy('pool', wab[:], wab32[:], [('wab32',)], [('wab',)])
    kb.copy('pool', idb[:], id32[:], [('id32',)], [('idb',)])
    kb.ts('dve', qg8[:], pp[:, PP_QG:PP_QG + 1], 0.125, None, ALU.mult, None, [('pp',)], [('qg8',)])
    kb.act(negA[:], tb[:, 8:16], AF.Exp, [('tb',)], [('negA',)])
    kb.ts('dve', negA[:], negA[:], -1.0, None, ALU.mult, None, [('negA',)], [('negA',)])
    kb.act(esnk[:], snk[:], AF.Exp, [('snk',)], [('esnk',)])
    kb.dma(kout_d[:, 2:6, :, 0:64], kcache_d[:, :, :, 64:128], [], [('kout', 'c')])
    kb.dma(vout_d[:, 2:6, :, 0:64], vcache_d[:, :, :, 64:128], [], [('vout', 'c')])
    kb.act(scT[:], cT_sb[:], AF.Silu, [('cT',)], [('scT',)])
    kb.copy('dve', scTb[:], scT[:], [('scT',)], [('scTb',)])
    NG = NMOD * D // 512
    for g in range(NG):
        wb = wada_sb[g % 2]
        kb.dma(wb, w_ada[:, :, g * 512:(g + 1) * 512], [AR], [('wada', g % 2)])
        wbb = wbuf[g % NWB][:, 0:4096].rearrange("p (k n) -> p k n", k=KC)
        if g % 2 == 0:
            kb.copy('dve', wbb, wb, [('wada', g % 2), AR], [('wbuf', g % NWB)])
        else:
            kb.act(wbb, wb, AF.Copy, [('wada', g % 2), AR], [('wbuf', g % NWB)])
        b = next_ps()
        kb.mm_group(ps[b][0:NSEQ, 0:512], [(scTb[:, kc, :], wbb[:, kc, :]) for kc in range(KC)],
                    [('wbuf', g % NWB), ('scTb',)], [('ps', b)])
        mtm = sg[g % 2][0:NSEQ, :]
        kb.act(mtm, ps[b][0:NSEQ, 0:512], AF.Copy, [('ps', b)], [('sg', g % 2)])
        b2 = next_ps()
        for j in range(4):
            kb.mm_group(ps[b2][:, j * NSEQ:(j + 1) * NSEQ], [(mtm[:, j * 128:(j + 1) * 128], c32[0:NSEQ, 3, 0:NSEQ])],
                        [('sg', g % 2), ('c32',)], [('ps', b2)])
        kb.tt('dve', modT[:, g * 4:(g + 1) * 4, :],
              ps[b2][:, 0:4 * NSEQ].rearrange("p (j s) -> p j s", j=4),
              bada_sb[:, g * 4:(g + 1) * 4].unsqueeze(2).to_broadcast([128, 4, NSEQ]),
              ALU.add, [('ps', b2), ('bada',)], [('modT',)])
    for i in range(3):
        sc = modT[:, (3 * i + 1) * KC:(3 * i + 2) * KC, :]
        gt = modT[:, (3 * i + 2) * KC:(3 * i + 3) * KC, :]
        kb.stt('dve', Amod[:, i, :, :], sc, 1.0,
               gains_sb[:, i, :].unsqueeze(2).to_broadcast([128, KC, NSEQ]),
               ALU.add, ALU.mult, [('modT',), ('gains',)], [('Amod',)])
        kb.ts('dve', Gmod[:, i, :, :], gt, 0.5 if i != 1 else 1.0, None, ALU.mult, None,
              [('modT',)], [('Gmod',)])

    junk = kb.tile("junk", [128, 4], F32)
    kb.memset('dve', junk[:, 0:1], 0.0, [AR])
    kb.memset('pool', junk[:, 1:2], 0.0, [AR])
    kb.act(junk[:, 2:3], epsc[:], AF.Copy, [('epsc',)], [AR])

    wrr = {'i': 0}
    def load_w(name, g, kc, ncols):
        ensure_cast(wpos[(name, g)] + LOOK)
        s = wrr['i'] % NWB
        wrr['i'] += 1
        kb.dma(wbuf[s][:, 0:kc * ncols], wscr[name][g].rearrange("p k n -> p (k n)"),
               [('wscr', name, g)], [('wbuf', s)])
        return s, wbuf[s][:, 0:kc * ncols].rearrange("p (k n) -> p k n", k=kc)

    def rsqrt_from_ps(dst, b, W, bias_ap=None, rd=(), npart=128):
        kb.act(dst, ps[b][0:npart, 0:W], AF.Ln, [('ps', b), ('epsc',)], list(rd), bias=epsc[0:npart, 0:1])
        if bias_ap is None:
            kb.act(dst, dst, AF.Exp, list(rd), list(rd), scale=-0.5)
        else:
            kb.act(dst, dst, AF.Exp, list(rd) + [('epsc',)], list(rd), scale=-0.5, bias=bias_ap)

    def norm_mod(i, c0, W, segs, alt=False):
        so = 256 if alt else 0
        ho = 256 if alt else 0
        hk = 'hT2' if alt else 'hT'
        rk = ('rstd2',) if alt else ('rstd',)
        tbuf = sq[:, :, 0:W] if alt else tmp[:, :, 0:W]
        tk_ = ('sq',) if alt else ('tmp',)
        kb.act(sq[:, :, so:so + W], x_sb[:, :, c0:c0 + W], AF.Square, [('x',)], [('sq',)])
        b = next_ps()
        pairs = [(ones[:], sq[:, kc, so:so + W]) for kc in range(KC)]
        kb.mm_group(ps[b][:, 0:W], pairs, [('ones',), ('sq',)], [('ps', b)])
        rsqrt_from_ps(rstd[:, so:so + W], b, W, rd=[rk])
        kb.tt('dve', tbuf, x_sb[:, :, c0:c0 + W],
              rstd[:, so:so + W].unsqueeze(1).to_broadcast([128, KC, W]), ALU.mult,
              [('x',), rk], [tk_])
        sh0 = 3 * i * KC
        for kc in range(KC):
            for (a, b_, s) in segs:
                kb.act(hT[:, kc, ho + a:ho + b_], tbuf[:, kc, a:b_], AF.Identity,
                       [tk_, ('Amod',), ('modT',)], [(hk, kc)],
                       bias=modT[:, sh0 + kc, s:s + 1], scale=Amod[:, i, kc, s:s + 1])

    def ffn(i, wi, wo, TW, segs):
        norm_mod(i, 0, TW, segs)
        for g in range(FC // 2):
            s, w = load_w(wi, g, KC, 512)
            for j in range(2):
                fc = 2 * g + j
                bg = next_ps()
                bu = next_ps()
                kb.mm_group(ps[bg][:, 0:TW], [(w[:, kc, j * 128:(j + 1) * 128], hT[:, kc, 0:TW]) for kc in range(KC)],
                            [('wbuf', s)] + [('hT', kc) for kc in range(KC)], [('ps', bg)])
                kb.mm_group(ps[bu][:, 0:TW], [(w[:, kc, 256 + j * 128:256 + (j + 1) * 128], hT[:, kc, 0:TW]) for kc in range(KC)],
                            [('wbuf', s)] + [('hT', kc) for kc in range(KC)], [('ps', bu)])
                sgt = sg[fc % 2]
                kb.act(sgt[:, 0:TW], ps[bg][:, 0:TW], AF.Silu, [('ps', bg)], [('sg', fc % 2)])
                kb.tt('dve', hid[:, fc, 0:TW], sgt[:, 0:TW], ps[bu][:, 0:TW], ALU.mult,
                      [('sg', fc % 2), ('ps', bu)], [('hid', fc)])
        for oc in range(KC):
            s, w = load_w(wo, oc, FC, 128)
            b = next_ps()
            kb.mm_group(ps[b][:, 0:TW], [(w[:, fc, :], hid[:, fc, 0:TW]) for fc in range(FC)],
                        [('wbuf', s)] + [('hid', fc) for fc in range(FC)], [('ps', b)])
            for (c0, c1, sq_) in segs:
                kb.stt('dve', x_sb[:, oc, c0:c1], ps[b][:, c0:c1], Gmod[:, i, oc, sq_:sq_ + 1],
                       x_sb[:, oc, c0:c1], ALU.mult, ALU.add,
                       [('ps', b), ('Gmod',), ('x',)], [('x',)])

    def mixer(m0, nseg, L, seq0, gch0, first, last, prenormed=False, after_proj=None):
        W = nseg * L
        NCH = W // 64
        cps = L // 64
        segs = [(j * L, (j + 1) * L, seq0 + j) for j in range(nseg)]
        HK = 128 + L
        HV = 2 + cps
        if first and nseg == 1:
            kb.memset('pool', S32[:], 0.0, [('S32', 0), ('S32', 1)])
            kb.memset('pool', Sbf[0][:], 0.0, [('Sbf', 0, 0), ('Sbf', 0, 1)])
        if nseg == 4:
            for j in range(4):
                st = kvst[:, 0:512].rearrange("p (s t) -> p s t", s=4)
                kb.dma(st, kcache_d[:, :, j, :], [], [('hid', 0), ('hid', 1), ('hid', 2), ('hid', 3)])
                kb.copy('pool', ksT[:, j, :].rearrange("p (s t) -> p s t", s=4)[:, :, 0:128], st,
                        [('hid', 0), ('hid', 1), ('hid', 2), ('hid', 3)], [('ksT', j)])
                st2 = kvst[0:64, 0:1024].rearrange("p (s c d) -> p s c d", s=4, c=2)
                kb.dma(st2, vcachetm_d[:, j, :, :, :], [], [('hid', 0), ('hid', 1), ('hid', 2), ('hid', 3)])
                kb.copy('pool', vtm[:, j, :].rearrange("p (s c d) -> p s c d", s=4, c=HV)[:, :, 0:2, :], st2,
                        [('hid', 0), ('hid', 1), ('hid', 2), ('hid', 3)], [('vtm', j)])
        if not prenormed:
            norm_mod(1, m0, W, segs)
        ho = 256 if prenormed else 0
        hk = 'hT2' if prenormed else 'hT'
        seqsl = slice(seq0, seq0 + nseg)
        pend = []
        for g in range(NPROJ // 4):
            s, w = load_w('win', g, KC, 512)
            for jj in range(4):
                ch = 4 * g + jj
                b = next_ps()
                kb.mm_group(ps[b][:, 0:W], [(w[:, kc, jj * 128:(jj + 1) * 128], hT[:, kc, ho:ho + W]) for kc in range(KC)],
                            [('wbuf', s)] + [(hk, kc) for kc in range(KC)], [('ps', b)])
                pv = ps[b][:, 0:W]
                while pend and pend[0][0] <= ch - 2:
                    pend.pop(0)[1]()
                if ch < 24:
                    r = ch % 2
                    rb = rawb[r][:, 0:nseg * (3 + L)].rearrange("p (s t) -> p s t", s=nseg)
                    kb.copy('pool', rb[:, :, 0:3], hist[:, ch, seqsl, :], [('hist',)], [('rawb', r)])
                    kb.copy('dve', rb[:, :, 3:3 + L], pv.rearrange("p (s t) -> p s t", s=nseg),
                            [('ps', b)], [('rawb', r)])
                    kb.copy('pool', hist[:, ch, seqsl, :], rb[:, :, L:L + 3], [('rawb', r)], [('hist',)])
                    acc = cacc[r][:, 0:W].rearrange("p (s t) -> p s t", s=nseg)
                    kb.act(acc, pv.rearrange("p (s t) -> p s t", s=nseg), AF.Copy, [('ps', b), ('pp',)], [('cacc', r)],
                           scale=pp[:, PP_CW + ch * 4 + 3:PP_CW + ch * 4 + 4])
                    for tap in (2, 1, 0):
                        kb.stt('dve', acc, rb[:, :, tap:tap + L], pp[:, PP_CW + ch * 4 + tap:PP_CW + ch * 4 + tap + 1],
                               acc, ALU.mult, ALU.add, [('rawb', r), ('pp',), ('cacc', r)], [('cacc', r)])
                    dst = qkT[:, ch, 0:W] if ch < 16 else vT[:, ch - 16, 0:W]
                    kb.act(dst, cacc[r][:, 0:W], AF.Silu, [('cacc', r)], [('qkv', ch)])
                elif ch < 32:
                    kb.act(zT[:, ch - 24, 0:W], pv, AF.Silu, [('ps', b)], [('zT', ch - 24)])
                elif ch < 44:
                    r = (ch // 2) % 2
                    hf = ch % 2
                    kb.act(sqb[r][:, hf * W:(hf + 1) * W], pv, AF.Square, [('ps', b)], [('sqb', r)])
                    kb.copy('dve', kn32[r][:, hf * W:(hf + 1) * W], pv, [('ps', b)], [('kn32', r)])
                    if hf == 1:
                        def fin(ch0=ch - 1, r=r):
                            b2 = next_ps()
                            for hh_ in range(2):
                                kb.mm_group(ps[b2][:, hh_ * W:(hh_ + 1) * W], [(idb[:, 1, :], sqb[r][:, hh_ * W:(hh_ + 1) * W])],
                                            [('idb',), ('sqb', r)], [('ps', b2)])
                            rsqrt_from_ps(rs32[r][:, 0:2 * W], b2, 2 * W, rd=[('rs32', r)])
                            if ch0 < 40:
                                c0_ = ch0 - 32
                                kb.stt('dve', qsT[:, c0_:c0_ + 2, 0:W], kn32[r][:, 0:2 * W].rearrange("p (a t) -> p a t", a=2),
                                       qg8[:, 0:1], rs32[r][:, 0:2 * W].rearrange("p (a t) -> p a t", a=2),
                                       ALU.mult, ALU.mult, [('kn32', r), ('rs32', r), ('qg8',)],
                                       [('qsT', c0_), ('qsT', c0_ + 1)])
                            else:
                                kb.stt('dve', kn32[r][:, 0:2 * W], kn32[r][:, 0:2 * W], pp[:, PP_KG:PP_KG + 1], rs32[r][:, 0:2 * W],
                                       ALU.mult, ALU.mult, [('kn32', r), ('rs32', r), ('pp',)], [('kn32', r)])
                                for hh_ in range(2):
                                    j = ch0 - 40 + hh_
                                    src = kn32[r][:, hh_ * W:(hh_ + 1) * W]
                                    kb.copy('pool', ksT[:, j, 0:nseg * HK].rearrange("p (s t) -> p s t", s=nseg)[:, :, 128:128 + L],
                                            src.rearrange("p (s t) -> p s t", s=nseg), [('kn32', r)], [('ksT', j)])
                                    if nseg == 4:
                                        kb.dma(kout_d[:, 2:6, j, 64:128], src.rearrange("p (s t) -> p s t", s=4),
                                               [('kn32', r)], [('kout', j)])
                                    elif last:
                                        kb.dma(kout_d[:, seq0, j, :], src[:, L - 128:L], [('kn32', r)], [('kout', seq0, j)])
                        pend.append((ch, fin))
                elif ch < 48:
                    j = ch - 44
                    kb.act(vsb[:, j, 0:W], pv, AF.Copy, [('ps', b)], [('vsb', j)])
                    if nseg == 4 or last:
                        r = ch % 2
                        kb.copy('dve', kn32[r][:, 256:256 + W], pv, [('ps', b)], [('kn32', r)])
                        if nseg == 4:
                            kb.dma(vout_d[:, 2:6, j, 64:128], kn32[r][:, 256:256 + W].rearrange("p (s t) -> p s t", s=4),
                                   [('kn32', r)], [('vout', j)])
                        else:
                            kb.dma(vout_d[:, seq0, j, :], kn32[r][:, 256 + L - 128:256 + L], [('kn32', r)], [('vout', seq0, j)])
                else:
                    f = ch - 48
                    kb.act(gates[:, f, 0:W], pv, AF.Sigmoid, [('ps', b), ('pp',)], [('gates', f)],
                           bias=pp[:, PP_BM + f:PP_BM + f + 1])
        while pend:
            pend.pop(0)[1]()
        if after_proj is not None:
            after_proj()
        if STOP <= 0:
            return
        pab = next_ps()
        for c in range(NCH):
            kb.mm_group(ps[pab][0:64, c * 16:(c + 1) * 16],
                        [(hT[:, kc, ho + c * 64:ho + (c + 1) * 64], wab[:, kc, :]) for kc in range(KC)],
                        [('wab',)] + [(hk, kc) for kc in range(KC)], [('ps', pab)])
        abv = ps[pab][0:64, 0:NCH * 16].rearrange("p (c k) -> p c k", c=NCH)
        G = {n: t[:, 0:NCH, :] for n, t in gs.items()}
        kb.tt('dve', G['xa'], abv[:, :, 0:8], tb[:, 0:8].unsqueeze(1).to_broadcast([64, NCH, 8]), ALU.add,
              [('ps', pab), ('tb',)], [('gs', 'xa')])
        kb.act(G['ea'], G['xa'], AF.Exp, [('gs', 'xa')], [('gs', 'ea')])
        kb.act(G['ea'], G['ea'], AF.Ln, [('gs', 'ea'), ('epsc',)], [('gs', 'ea')], bias=onec[0:64, 0:1])
        kb.tt('dve', G['g'], G['ea'], negA[:].unsqueeze(1).to_broadcast([64, NCH, 8]), ALU.mult,
              [('gs', 'ea'), ('negA',)], [('gs', 'g')])
        kb.act(G['eb'], abv[:, :, 8:16], AF.Exp, [('ps', pab)], [('gs', 'eb')], scale=-1.0)
        kb.ts('dve', G['eb'], G['eb'], 1.0, None, ALU.add, None, [('gs', 'eb')], [('gs', 'eb')])

        def recip(eng_, out, in_, reads, writes):
            def fn(e):
                return e.reciprocal(out=out, in_=in_)
            return P.op(eng_, fn, reads, writes)
        recip('dve', G['beta'], G['eb'], [('gs', 'eb')], [('gs', 'beta')])
        pg = next_ps()
        gflat = gs['g'][:, 0:NCH, :].rearrange("p c h -> p (c h)")
        kb.mm_group(ps[pg][0:64, 0:NCH * 8], [(c32[:, 0, :], gflat)], [('c32',), ('gs', 'g')], [('ps', pg)])
        kb.mm_group(ps[pg][:, 64:64 + NCH * 8], [(onesf[:, :], gflat)], [('ones',), ('gs', 'g')], [('ps', pg)])
        Gps = ps[pg][0:64, 0:NCH * 8].rearrange("p (c h) -> p c h", c=NCH)
        GLps = ps[pg][:, 64:64 + NCH * 8].rearrange("p (c h) -> p c h", c=NCH)
        kb.act(G['Gs'], Gps, AF.Copy, [('ps', pg)], [('gs', 'Gs')])
        kb.act(G['eG'], Gps, AF.Exp, [('ps', pg)], [('gs', 'eG')])
        kb.act(dlast[:, 0:NCH, :], GLps, AF.Exp, [('ps', pg)], [('dlast',)])
        kb.tt('dve', G['dd'], GLps[0:64], G['Gs'], ALU.subtract, [('ps', pg), ('gs', 'Gs')], [('gs', 'dd')])
        kb.act(G['dec'], G['dd'], AF.Exp, [('gs', 'dd')], [('gs', 'dec')])
        kb.tt('dve', G['bG'], G['beta'], G['eG'], ALU.mult, [('gs', 'beta'), ('gs', 'eG')], [('gs', 'bG')])
        for fp in range(8):
            f0 = 2 * fp
            r = fp % 2
            qv = qkT[:, f0:f0 + 2, 0:W]
            kb.tt('pool', sqb[r][:, 0:2 * W].rearrange("p (a t) -> p a t", a=2), qv, qv, ALU.mult,
                  [('qkv', f0), ('qkv', f0 + 1)], [('sqb', r)])
            b2 = next_ps()
            for hh_ in range(2):
                kb.mm_group(ps[b2][:, hh_ * W:(hh_ + 1) * W], [(ones1[:], sqb[r][:, hh_ * W:(hh_ + 1) * W])],
                            [('ones',), ('sqb', r)], [('ps', b2)])
            rsqrt_from_ps(rs32[r][:, 0:2 * W], b2, 2 * W, bias_ap=(lnq[:, 0:1] if f0 < 8 else None), rd=[('rs32', r)])
            kb.tt('dve', qv, qv, rs32[r][:, 0:2 * W].rearrange("p (a t) -> p a t", a=2), ALU.mult,
                  [('qkv', f0), ('qkv', f0 + 1), ('rs32', r)], [('qkv', f0), ('qkv', f0 + 1)])
        for j in range(4):
            b = next_ps()
            for c in range(NCH):
                kb.mm_group(ps[b][0:64, c * 128:(c + 1) * 128], [(vsb[:, j, c * 64:(c + 1) * 64], idb[:, 0, :])],
                            [('vsb', j), ('idb',)], [('ps', b)])
            kb.copy('dve', vtm[:, j, 0:nseg * HV * 128].rearrange("p (s c d) -> p s c d", s=nseg, c=HV)[:, :, 2:2 + cps, :],
                    ps[b][0:64, 0:NCH * 128].rearrange("p (s c d) -> p s c d", s=nseg, c=cps),
                    [('ps', b)], [('vtm', j)])
        if STOP <= 1:
            return
        qk_reads = [('qkv', f) for f in range(24)]
        HGN = 2
        HW_ = 256
        I3 = c32[:, 3, :].unsqueeze(1).to_broadcast([64, 4, 64])

        def v3(ap):
            return ap.rearrange("p (h j) -> p h j", h=4)

        def gdn_gen():
            for c in range(NCH):
                tk = slice(c * 64, (c + 1) * 64)
                seq = seq0 + (c if nseg == 4 else 0)
                if nseg == 4:
                    kb.dma(S32[:], Sin_d[:, c, :, :], [], [('S32', 0), ('S32', 1)])
                    kb.copy('pool', Sbf[0][:], S32[:], [('S32', 0), ('S32', 1)], [('Sbf', 0, 0), ('Sbf', 0, 1)])
                kb.tt('pool', Pc[:], gs['g'][:, c, :].unsqueeze(2).to_broadcast([64, 8, 64]),
                      c32[:, 1, :].unsqueeze(1).to_broadcast([64, 8, 64]), ALU.mult,
                      [('gs', 'g'), ('c32',)], [('hid', 4), ('hid', 5)])
                kb.tt('pool', BS[:], gs['beta'][:, c, :].unsqueeze(2).to_broadcast([64, 8, 64]),
                      c32[:, 1, :].unsqueeze(1).to_broadcast([64, 8, 64]), ALU.mult,
                      [('gs', 'beta'), ('c32',)], [('hid', 8), ('hid', 9)])
                kb.tt('pool', rhsE[:], gs['eG'][:, c, :].unsqueeze(2).to_broadcast([64, 8, 64]),
                      c32[:, 3, :].unsqueeze(1).to_broadcast([64, 8, 64]), ALU.mult,
                      [('gs', 'eG'), ('c32',)], [('hid', 10), ('hid', 11)])
                Pcf = Pc[:].rearrange("p h j -> p (h j)")
                BSf = BS[:].rearrange("p h j -> p (h j)")
                rhsEf = rhsE[:].rearrange("p h j -> p (h j)")
                HS = [slice(hg * HW_, (hg + 1) * HW_) for hg in range(HGN)]
                bD, bkk, bqk = {}, {}, {}
                for hg in range(HGN):
                    bD[hg], bkk[hg], bqk[hg] = next_ps(), next_ps(), next_ps()
                    kb.mm_group(ps[bD[hg]][0:64, 0:HW_], [(c32[:, 0, :], Pcf[:, HS[hg]])],
                                [('c32',), ('hid', 4), ('hid', 5)], [('ps', bD[hg])])
                    for hh in range(4):
                        h = hg * 4 + hh
                        kb.mm_group(ps[bkk[hg]][0:64, hh * 64:(hh + 1) * 64], [(qkT[:, 8 + h, tk], qkT[:, 8 + h, tk])],
                                    qk_reads, [('ps', bkk[hg])])
                    for hh in range(4):
                        h = hg * 4 + hh
                        kb.mm_group(ps[bqk[hg]][0:64, hh * 64:(hh + 1) * 64], [(qkT[:, h, tk], qkT[:, 8 + h, tk])],
                                    qk_reads, [('ps', bqk[hg])])
                for hg in range(HGN):
                    kb.act(Em[:, HS[hg]], ps[bD[hg]][0:64, 0:HW_], AF.Exp, [('ps', bD[hg])], [('Em', hg)])
                for hg in range(HGN):
                    kb.tt('dve', tA[0][:, HS[hg]], ps[bkk[hg]][0:64, 0:HW_], Em[:, HS[hg]], ALU.mult,
                          [('ps', bkk[hg]), ('Em', hg)], [('tA', 0, hg)])
                    kb.tt('dve', Nb[0][:, HS[hg]], tA[0][:, HS[hg]], BSf[:, HS[hg]], ALU.mult,
                          [('tA', 0, hg), ('hid', 8), ('hid', 9)], [('Nb', 0, hg)])
                    kb.tt('pool', v3(INb[:, HS[hg]]), v3(Nb[0][:, HS[hg]]), I3, ALU.add,
                          [('Nb', 0, hg), ('c32',)], [('INb', hg), ('hid', 12)])
                    kb.tt('dve', tA[1][:, HS[hg]], ps[bqk[hg]][0:64, 0:HW_], Em[:, HS[hg]], ALU.mult,
                          [('ps', bqk[hg]), ('Em', hg)], [('tA', 1, hg)])
                    kb.tt('pool', v3(QKb[:, HS[hg]]), v3(tA[1][:, HS[hg]]),
                          c32[:, 2, :].unsqueeze(1).to_broadcast([64, 4, 64]), ALU.mult,
                          [('tA', 1, hg), ('c32',)], [('QKb', hg)])
                yield
                for outs in ('k', 'v'):
                    for hg in range(HGN):
                        b = next_ps()
                        for hh in range(4):
                            h = hg * 4 + hh
                            srcT = qkT[:, 8 + h, tk] if outs == 'k' else vT[:, h, tk]
                            kb.mm_group(ps[b][0:64, hh * 128:(hh + 1) * 128], [(srcT, idb[:, 0, :])],
                                        qk_reads + [('idb',)], [('ps', b)])
                        pvw = ps[b][0:64, :].rearrange("p (h d) -> p h d", h=4)
                        hsl = slice(hg * 4, hg * 4 + 4)
                        if outs == 'v':
                            kb.tt('dve', vbt[0:64, hsl, :], pvw, gs['beta'][:, c, hsl].unsqueeze(2).to_broadcast([64, 4, 128]),
                                  ALU.mult, [('ps', b), ('gs', 'beta')], [('vbt', hg)])
                        else:
                            kb.tt('dve', kbt[:, hsl, :], pvw, gs['bG'][:, c, hsl].unsqueeze(2).to_broadcast([64, 4, 128]),
                                  ALU.mult, [('ps', b), ('gs', 'bG')], [('kbt', hg)])
                            kb.tt('dve', kdt[:, hsl, :], pvw, gs['dec'][:, c, hsl].unsqueeze(2).to_broadcast([64, 4, 128]),
                                  ALU.mult, [('ps', b), ('gs', 'dec')], [('kdt', hg)])
                for hg in range(HGN):
                    bE = next_ps()
                    kb.mm_group(ps[bE][:, 0:HW_], [(onesf[:, :], rhsEf[:, HS[hg]])],
                                [('ones',), ('hid', 10), ('hid', 11)], [('ps', bE)])
                    kb.tt('dve', v3(qdecT[:, HS[hg]]), qkT[:, hg * 4:hg * 4 + 4, tk], v3(ps[bE][:, 0:HW_]), ALU.mult,
                          qk_reads + [('ps', bE)], [('qdecT', hg)])
                yield
                bM, bQ = {}, {}
                for hg in range(HGN):
                    bM[hg], bQ[hg] = next_ps(), next_ps()
                    for hh in range(4):
                        cs = slice(hg * HW_ + hh * 64, hg * HW_ + (hh + 1) * 64)
                        kb.mm_group(ps[bM[hg]][0:64, hh * 64:(hh + 1) * 64], [(Nb[0][:, cs], idb[0:64, 0, 0:64])],
                                    [('Nb', 0, hg), ('idb',)], [('ps', bM[hg])])
                    for hh in range(4):
                        cs = slice(hg * HW_ + hh * 64, hg * HW_ + (hh + 1) * 64)
                        kb.mm_group(ps[bQ[hg]][0:64, hh * 64:(hh + 1) * 64], [(QKb[:, cs], idb[0:64, 0, 0:64])],
                                    [('QKb', hg), ('idb',)], [('ps', bQ[hg])])
                for hg in range(HGN):
                    kb.act(Mb[0][:, HS[hg]], ps[bM[hg]][0:64, 0:HW_], AF.Copy, [('ps', bM[hg])], [('Mb', 0, hg)])
                    kb.tt('dve', v3(Rb[0][0:64, HS[hg]]), I3, v3(ps[bM[hg]][0:64, 0:HW_]), ALU.subtract,
                          [('c32',), ('ps', bM[hg])], [('Rb', 0, hg)])
                    kb.act(QKT[0:64, HS[hg]], ps[bQ[hg]][0:64, 0:HW_], AF.Copy, [('ps', bQ[hg])], [('QKT', hg)])
                yield
                cur = 0
                rc = 0
                for lev in range(1, 7):
                    nxt = 1 - cur
                    bN, bM2, bR = {}, {}, {}
                    for hg in range(HGN):
                        if lev <= 5:
                            bN[hg] = next_ps()
                            for hh in range(4):
                                cs = slice(hg * HW_ + hh * 64, hg * HW_ + (hh + 1) * 64)
                                kb.mm_group(ps[bN[hg]][0:64, hh * 64:(hh + 1) * 64], [(Mb[cur][:, cs], Nb[cur][:, cs])],
                                            [('Mb', cur, hg), ('Nb', cur, hg)], [('ps', bN[hg])])
                        if lev <= 4:
                            bM2[hg] = next_ps()
                            for hh in range(4):
                                cs = slice(hg * HW_ + hh * 64, hg * HW_ + (hh + 1) * 64)
                                kb.mm_group(ps[bM2[hg]][0:64, hh * 64:(hh + 1) * 64], [(Nb[cur][:, cs], Mb[cur][:, cs])],
                                            [('Mb', cur, hg), ('Nb', cur, hg)], [('ps', bM2[hg])])
                        if lev >= 2:
                            bR[hg] = next_ps()
                            for hh in range(4):
                                cs = slice(hg * HW_ + hh * 64, hg * HW_ + (hh + 1) * 64)
                                kb.mm_group(ps[bR[hg]][0:64, hh * 64:(hh + 1) * 64], [(Nb[cur][:, cs], Rb[rc][0:64, cs])],
                                            [('Nb', cur, hg), ('Rb', rc, hg)], [('ps', bR[hg])])
                    for hg in range(HGN):
                        if lev <= 5:
                            kb.act(Nb[nxt][:, HS[hg]], ps[bN[hg]][0:64, 0:HW_], AF.Copy, [('ps', bN[hg])], [('Nb', nxt, hg)])
                        if lev <= 4:
                            kb.act(Mb[nxt][:, HS[hg]], ps[bM2[hg]][0:64, 0:HW_], AF.Copy, [('ps', bM2[hg])], [('Mb', nxt, hg)])
                        if lev >= 2:
                            kb.tt('dve', Rb[1 - rc][0:64, HS[hg]], ps[bR[hg]][0:64, 0:HW_], Rb[rc][0:64, HS[hg]], ALU.add,
                                  [('ps', bR[hg]), ('Rb', rc, hg)], [('Rb', 1 - rc, hg)])
                    if lev >= 2:
                        rc = 1 - rc
                    cur = nxt
                    yield
                Tt = Rb[rc]
                yield
                bT, bF = {}, {}
                for hg in range(HGN):
                    bT[hg], bF[hg] = next_ps(), next_ps()
                    for hh in range(4):
                        cs = slice(hg * HW_ + hh * 64, hg * HW_ + (hh + 1) * 64)
                        kb.mm_group(ps[bT[hg]][0:64, hh * 64:(hh + 1) * 64], [(Tt[0:64, cs], idb[0:64, 0, 0:64])],
                                    [('Rb', rc, hg), ('idb',)], [('ps', bT[hg])])
                    for hh in range(4):
                        cs = slice(hg * HW_ + hh * 64, hg * HW_ + (hh + 1) * 64)
                        kb.mm_group(ps[bF[hg]][0:64, hh * 64:(hh + 1) * 64], [(INb[:, cs], Tt[0:64, cs])],
                                    [('Rb', rc, hg), ('INb', hg), ('hid', 12)], [('ps', bF[hg])])
                for hg in range(HGN):
                    kb.act(Tn[:, HS[hg]], ps[bT[hg]][0:64, 0:HW_], AF.Copy, [('ps', bT[hg])], [('Tn', hg)])
                    kb.tt('dve', v3(Fb[:, HS[hg]]), I3, v3(ps[bF[hg]][0:64, 0:HW_]), ALU.subtract,
                          [('c32',), ('ps', bF[hg])], [('Fb', hg)])
                yield
                bC = {}
                for hg in range(HGN):
                    bC[hg] = next_ps()
                    for hh in range(4):
                        cs = slice(hg * HW_ + hh * 64, hg * HW_ + (hh + 1) * 64)
                        kb.mm_group(ps[bC[hg]][0:64, hh * 64:(hh + 1) * 64], [(Tn[:, cs], Fb[:, cs])],
                                    [('Tn', hg), ('Fb', hg)], [('ps', bC[hg])])
                for hg in range(HGN):
                    kb.act(Cb[0:64, HS[hg]], ps[bC[hg]][0:64, 0:HW_], AF.Copy, [('ps', bC[hg])], [('Cb', hg)])
                yield
                bw = {}
                for hg in range(HGN):
                    bw[hg] = next_ps()
                    for hh in range(4):
                        h = hg * 4 + hh
                        cs = slice(hg * HW_ + hh * 64, hg * HW_ + (hh + 1) * 64)
                        kb.mm_group(ps[bw[hg]][:, hh * 64:(hh + 1) * 64],
                                    [(kbt[:, h, :], Tt[0:64, cs]), (kbt[:, h, :], Cb[0:64, cs])],
                                    [('kbt', hg), ('Rb', rc, hg), ('Cb', hg)], [('ps', bw[hg])])
                for hg in range(HGN):
                    kb.act(negwT[:, HS[hg]], ps[bw[hg]][:, 0:HW_], AF.Copy, [('ps', bw[hg])], [('negwT', hg)], scale=-1.0)
                yield
                sc_ = c % 2 if nseg == 1 else 0
                nsc = (1 - sc_) if nseg == 1 else 0
                Sc = Sbf[sc_]
                bv, bo, bS = {}, {}, {}
                for hg in range(HGN):
                    bv[hg] = next_ps()
                    for hh in range(4):
                        h = hg * 4 + hh
                        cs = slice(hg * HW_ + hh * 64, hg * HW_ + (hh + 1) * 64)
                        kb.mm_group(ps[bv[hg]][0:64, hh * 128:(hh + 1) * 128],
                                    [(Tt[:, cs], vbt[:, h, :]), (Cb[:, cs], vbt[:, h, :]), (negwT[:, cs], Sc[:, h, :])],
                                    [('Rb', rc, hg), ('Cb', hg), ('vbt', hg), ('negwT', hg), ('Sbf', sc_, hg)],
                                    [('ps', bv[hg])])
                for hg in range(HGN):
                    kb.act(vnew[0:64, hg * 4:hg * 4 + 4, :], ps[bv[hg]][0:64, :].rearrange("p (h d) -> p h d", h=4),
                           AF.Copy, [('ps', bv[hg])], [('vnew', hg)])
                yield
                for hg in range(HGN):
                    bS[hg] = next_ps()
                    for hh in range(4):
                        h = hg * 4 + hh
                        kb.mm_group(ps[bS[hg]][:, hh * 128:(hh + 1) * 128], [(kdt[:, h, :], vnew[0:64, h, :])],
                                    [('kdt', hg), ('vnew', hg)], [('ps', bS[hg])])
                    bo[hg] = next_ps()
                    for hh in range(4):
                        h = hg * 4 + hh
                        cs = slice(hg * HW_ + hh * 64, hg * HW_ + (hh + 1) * 64)
                        kb.mm_group(ps[bo[hg]][:, hh * 64:(hh + 1) * 64], [(Sc[:, h, :], qdecT[:, cs]), (vnew[:, h, :], QKT[:, cs])],
                                    [('Sbf', sc_, hg), ('qdecT', hg), ('vnew', hg), ('QKT', hg)], [('ps', bo[hg])])
                for hg in range(HGN):
                    hsl = slice(hg * 4, hg * 4 + 4)
                    kb.tt('pool', S32[:, hsl, :], S32[:, hsl, :], dlast[:, c, hsl].unsqueeze(2).to_broadcast([128, 4, 128]),
                          ALU.mult, [('S32', hg), ('dlast',)], [('S32', hg)])
                    kb.tt('dve', S32[:, hsl, :], S32[:, hsl, :], ps[bS[hg]][:, :].rearrange("p (h d) -> p h d", h=4), ALU.add,
                          [('S32', hg), ('ps', bS[hg])], [('S32', hg)])
                    kb.copy('pool', Sbf[nsc][:, hsl, :], S32[:, hsl, :], [('S32', hg)], [('Sbf', nsc, hg)])
                    kb.act(oTb[:, hsl, tk], ps[bo[hg]][:, 0:HW_].rearrange("p (h j) -> p h j", h=4), AF.Copy,
                           [('ps', bo[hg])], [('tmp',)])
                if nseg == 4 or (last and c == NCH - 1):
                    kb.dma(Sout_d[:, seq, :, :], S32[:], [('S32', 0), ('S32', 1)], [('Sout', seq)])
                yield
        def swa_A(c, j, pr):
            tk = slice(c * 64, (c + 1) * 64)
            sgi = c if nseg == 4 else 0
            lc = 0 if nseg == 4 else c
            gch = gch0 + lc
            rvalid = [r for r in range(3) if (nseg == 4 or gch - 2 + r >= 0)]
            bs_ = [next_ps(), next_ps()]
            for r in rvalid:
                kcol = sgi * HK + (lc + r) * 64
                for g in range(4):
                    par, a = g % 2, g // 2
                    base = par * 64
                    kb.mm_group(ps[bs_[par]][0:64, r * 128 + a * 64:r * 128 + (a + 1) * 64],
                                [(ksT[base:base + 64, j, kcol:kcol + 64], qsT[base:base + 64, 2 * j + a, tk])],
                                [('ksT', j), ('qsT', 2 * j + a)], [('ps', bs_[par])])
            r0, r1 = min(rvalid) * 128, (max(rvalid) + 1) * 128
            for par in range(2):
                kb.act(pT[pr][:, par * 384 + r0:par * 384 + r1], ps[bs_[par]][0:64, r0:r1], AF.Exp,
                       [('ps', bs_[par])], [('pT', pr)])
            return rvalid

        def swa_B(c, j, pr, rvalid):
            tk = slice(c * 64, (c + 1) * 64)
            sgi = c if nseg == 4 else 0
            lc = 0 if nseg == 4 else c
            bo = next_ps()
            for par in range(2):
                kb.mm_group(ps[bo][:, par * 128:(par + 1) * 128],
                            [(vtm[:, j, (sgi * HV + lc + r) * 128:(sgi * HV + lc + r + 1) * 128],
                              pT[pr][:, par * 384 + r * 128:par * 384 + (r + 1) * 128]) for r in rvalid],
                            [('vtm', j), ('pT', pr)], [('ps', bo)])
            for par in range(2):
                kb.mm_group(ps[bo][:, 256 + par * 128:256 + (par + 1) * 128],
                            [(ones1[0:64, :], pT[pr][:, par * 384 + r * 128:par * 384 + (r + 1) * 128]) for r in rvalid],
                            [('ones',), ('pT', pr)], [('ps', bo)])
            kb.tt('dve', rden[pr][:].rearrange("p (g l) -> p g l", g=4),
                  ps[bo][:, 256:512].rearrange("p (g l) -> p g l", g=4),
                  esnk[:, j * 4:(j + 1) * 4].unsqueeze(2).to_broadcast([128, 4, 64]), ALU.add,
                  [('ps', bo), ('esnk',)], [('rden', pr)])
            recip('dve', rden[pr][:], rden[pr][:], [('rden', pr)], [('rden', pr)])
            for par in range(2):
                base = par * 64
                kb.tt('dve', osT[base:base + 64, 2 * j:2 * j + 2, tk],
                      ps[bo][base:base + 64, par * 128:(par + 1) * 128].rearrange("p (a l) -> p a l", a=2),
                      rden[pr][base:base + 64, par * 128:(par + 1) * 128].rearrange("p (a l) -> p a l", a=2), ALU.mult,
                      [('ps', bo), ('rden', pr)], [('osT', 2 * j), ('osT', 2 * j + 1)])

        def swa_gen():
            units = [(c, j) for c in range(NCH) for j in range(4)]
            pend = None
            for u, (c, j) in enumerate(units):
                rv = swa_A(c, j, u % 2)
                yield
                if pend is not None:
                    swa_B(*pend)
                    yield
                pend = (c, j, u % 2, rv)
            swa_B(*pend)
            yield

        gens = [gdn_gen()] + ([swa_gen()] if STOP > 3 else [])
        while gens:
            for g_ in list(gens):
                try:
                    next(g_)
                except StopIteration:
                    gens.remove(g_)
        if STOP <= 2:
            return
        for hp in range(4):
            h0 = 2 * hp
            r = hp % 2
            ov = oTb[:, h0:h0 + 2, 0:W]
            kb.act(sqb[r][:, 0:2 * W].rearrange("p (a t) -> p a t", a=2), ov, AF.Square, [('tmp',)], [('sqb', r)])
            b2 = next_ps()
            for hh_ in range(2):
                kb.mm_group(ps[b2][:, hh_ * W:(hh_ + 1) * W], [(ones128[:], sqb[r][:, hh_ * W:(hh_ + 1) * W])],
                            [('ones',), ('sqb', r)], [('ps', b2)])
            rsqrt_from_ps(rs32[r][:, 0:2 * W], b2, 2 * W, rd=[('rs32', r)])
            kb.tt('dve', ov, ov, rs32[r][:, 0:2 * W].rearrange("p (a t) -> p a t", a=2), ALU.mult,
                  [('tmp',), ('rs32', r)], [('tmp',)])
            kb.stt('dve', ov, ov, pp[:, PP_GN:PP_GN + 1], zT[:, h0:h0 + 2, 0:W], ALU.mult, ALU.mult,
                   [('tmp',), ('pp',), ('zT', h0), ('zT', h0 + 1)], [('tmp',)])
        if STOP <= 4:
            return
        if nseg == 1 and not last:
            for j in range(4):
                kb.copy('pool', ksT[:, j, 0:128], ksT[:, j, L:L + 128], [('ksT', j)], [('ksT', j)])
                kb.copy('pool', vtm[:, j, 0:256], vtm[:, j, cps * 128:(cps + 2) * 128], [('vtm', j)], [('vtm', j)])
        for f in range(8):
            kb.tt('dve', mg[:, 0:W], oTb[:, f, 0:W], gates[:, f, 0:W], ALU.mult, [('tmp',), ('gates', f)], [('mg',)])
            kb.tt('dve', osT[:, f, 0:W], osT[:, f, 0:W], gates[:, 8 + f, 0:W], ALU.mult,
                  [('osT', f), ('gates', 8 + f)], [('osT', f)])
            kb.tt('dve', hT[:, f, 0:W], mg[:, 0:W], osT[:, f, 0:W], ALU.add, [('mg',), ('osT', f)], [('hT', f)])
        for g in range(2):
            s, w = load_w('wout', g, KC, 512)
            for jj in range(4):
                oc = 4 * g + jj
                b = next_ps()
                kb.mm_group(ps[b][:, 0:W], [(w[:, kc, jj * 128:(jj + 1) * 128], hT[:, kc, 0:W]) for kc in range(KC)],
                            [('wbuf', s)] + [('hT', kc) for kc in range(KC)], [('ps', b)])
                for (a, b_, sq_) in segs:
                    kb.stt('dve', x_sb[:, oc, m0 + a:m0 + b_], ps[b][:, a:b_], Gmod[:, 1, oc, sq_:sq_ + 1],
                           x_sb[:, oc, m0 + a:m0 + b_], ALU.mult, ALU.add,
                           [('ps', b), ('Gmod',), ('x',)], [('x',)])

    xTv = xT.rearrange("(k p) t -> p k t", p=128)
    yTv = yT.rearrange("(k p) t -> p k t", p=128)
    for (t0, TW, segs, msubs) in tiles:
        kb.dma(x_sb[:, :, 0:TW], xTv[:, :, t0:t0 + TW], [], [('x',)])
        ffn(0, 'f1i', 'f1o', TW, segs)
        if do_mixer:
            if len(msubs) == 2 and cfg.get('hoist', True):
                ms0, ms1 = msubs
                (m1, ns1, L1, sq1) = ms1[0:4]
                segs1 = [(j * L1, (j + 1) * L1, sq1 + j) for j in range(ns1)]
                mixer(*ms0, after_proj=lambda: norm_mod(1, m1, ns1 * L1, segs1, alt=True))
                mixer(*ms1, prenormed=True)
            else:
                for ms in msubs:
                    mixer(*ms)
        ffn(2, 'f2i', 'f2o', TW, segs)
        kb.dma(yTv[:, :, t0:t0 + TW], x_sb[:, :, 0:TW], [('x',)], [('yT', t0)])
    kb.dma(histout_d[:, :, :, :], hist[:], [('hist',)], [('histout',)])

    P.emit(kb.es)
    kb.es.close()
    return nc


def make_tiles(n_prompt_tiles=4, sample=True, seqs=(0, 1)):
    tiles = []
    for s in seqs:
        for i in range(n_prompt_tiles):
            msubs = []
            for hh in range(2):
                msubs.append((hh * 256, 1, 256, s, (i * 2 + hh) * 4, (i == 0 and hh == 0),
                              (i == TP // 512 - 1 and hh == 1)))
            tiles.append((s * TP + i * 512, 512, [(0, 512, s)], msubs))
    if sample:
        tiles.append((2 * TP, 4 * TS, [(j * TS, (j + 1) * TS, 2 + j) for j in range(4)],
                      [(0, 4, 64, 2, 0, True, True)]))
    return tiles


def host_win_cols():
    cols = []
    for ch in range(24):
        cols.append(np.arange(ch * 128, (ch + 1) * 128))
    for h in range(8):
        cols.append(3072 + np.arange(h * 128, (h + 1) * 128))
    for i in range(8):
        cols.append(4112 + np.arange(i * 128, (i + 1) * 128))
    for j in range(4):
        c = 5136 + np.arange(j * 64, (j + 1) * 64)
        cols.append(np.concatenate([c, c]))
    for j in range(4):
        c = 5392 + np.arange(j * 64, (j + 1) * 64)
        cols.append(np.concatenate([c, c]))
    for f in range(16):
        cols.append(5648 + np.arange(f * 128, (f + 1) * 128))
    return cols


def host_inputs(inp, core, cache):
    f = np.float32
    m = {}
    xp = inp['x_prompt'][2 * core:2 * core + 2].reshape(2 * TP, D)
    xs = inp['x_sample'][4 * core:4 * core + 4].reshape(4 * TS, D)
    m['xT'] = np.ascontiguousarray(np.concatenate([xp, xs], 0).T.astype(f))
    c = np.concatenate([inp['c_prompt'][2 * core:2 * core + 2], inp['c_sample'][4 * core:4 * core + 4]], 0)
    m['cT'] = np.ascontiguousarray(c.T.reshape(KC, 128, NSEQ).transpose(1, 0, 2).astype(f))
    sl = slice(4 * core, 4 * core + 4)
    S = inp['state_gdn'][0, sl]
    m['Sin'] = np.ascontiguousarray(S.transpose(2, 0, 1, 3).astype(f))
    cs = inp['state_gdn_conv'][0, sl]
    m['histin'] = np.ascontiguousarray(cs.reshape(4, 3, 24, 128).transpose(3, 2, 0, 1).astype(f))
    kc_ = inp['cache_swa_k'][0, sl]
    vc_ = inp['cache_swa_v'][0, sl]
    kfm = kc_.transpose(3, 0, 2, 1)
    m['kcache'] = np.ascontiguousarray(np.concatenate([kfm, kfm], 0).astype(f))
    vfm = vc_.transpose(3, 0, 2, 1)
    m['vcache'] = np.ascontiguousarray(np.concatenate([vfm, vfm], 0).astype(f))
    vt = vc_.reshape(4, 2, 64, 4, 64).transpose(2, 3, 0, 1, 4)
    m['vcachetm'] = np.ascontiguousarray(np.concatenate([vt, vt], -1).astype(f))
    if 'shared' not in cache:
        sh = {}
        sh['w_ada'] = np.ascontiguousarray(inp['w_ada'][0].reshape(KC, 128, NMOD * D).transpose(1, 0, 2))
        sh['b_ada'] = np.ascontiguousarray(inp['b_ada'][0].reshape(NMOD * KC, 128).T)
        g = np.stack([inp['norm_ffn1'][0], inp['norm_mix'][0], inp['norm_ffn2'][0]], 0)
        sh['gains'] = np.ascontiguousarray(g.reshape(3, KC, 128).transpose(2, 0, 1))
        sh['f1i'] = host_group_weights(inp['ffn1_w_in'][0], ffn_in_groups(), KC)
        sh['f1o'] = host_group_weights(inp['ffn1_w_out'][0], ffn_out_groups(), FC)
        sh['f2i'] = host_group_weights(inp['ffn2_w_in'][0], ffn_in_groups(), KC)
        sh['f2o'] = host_group_weights(inp['ffn2_w_out'][0], ffn_out_groups(), FC)
        wc = host_win_cols()
        sh['win'] = host_group_weights(inp['w_in'][0], [np.concatenate(wc[4 * g:4 * g + 4]) for g in range(16)], KC)
        sh['wout'] = host_group_weights(inp['w_out'][0], [np.arange(g * 512, (g + 1) * 512) for g in range(2)], KC)
        sh['wab'] = np.ascontiguousarray(inp['w_in'][0][:, 4096:4112].reshape(KC, 128, 16).transpose(1, 0, 2))
        i_ = np.arange(64)
        c32 = np.zeros((64, 5, 64), f)
        c32[:, 0, :] = (i_[:, None] <= i_[None, :])
        c32[:, 1, :] = (i_[:, None] > i_[None, :])
        c32[:, 2, :] = (i_[None, :] <= i_[:, None])
        c32[:, 3, :] = np.eye(64)
        c32[:, 4, :] = 1.0
        sh['c32'] = c32
        id128 = np.zeros((128, 2, 128), f)
        id128[:, 0, :] = np.eye(128)
        id128[0:64, 1, 0:64] = 1.0 / 64
        id128[64:128, 1, 64:128] = 1.0 / 64
        sh['id128'] = id128
        pp = np.zeros((128, NPP), f)
        cw = inp['gdn_conv_w'][0]
        pp[:, PP_CW:PP_CW + 96] = cw.reshape(4, 24, 128).transpose(2, 1, 0).reshape(128, 96)
        pp[:, PP_BM:PP_BM + 16] = inp['b_merge'][0].reshape(16, 128).T
        pp[:, PP_GN] = inp['gdn_norm'][0]
        pp[:, PP_QG] = np.tile(inp['swa_q_norm'][0], 2)
        pp[:, PP_KG] = np.tile(inp['swa_k_norm'][0], 2)
        sh['pp'] = pp
        tb = np.zeros((64, 16), f)
        tb[:, 0:8] = inp['gdn_dt_bias'][0][None, :]
        tb[:, 8:16] = inp['gdn_a_log'][0][None, :]
        sh['tb'] = tb
        sk = inp['swa_sinks'][0].reshape(4, 2, 2).transpose(0, 2, 1).reshape(16)
        sh['snk'] = np.ascontiguousarray(np.broadcast_to(sk[None, :], (128, 16)).astype(f))
        cache['shared'] = sh
    m.update(cache['shared'])
    return m


def run(inputs, cfg):
    inputs = {k: np.asarray(v) for k, v in inputs.items()}
    nc = build(cfg)
    cache = {}
    in_maps = [host_inputs(inputs, c, cache) for c in range(NCORES)]
    if not cfg.get('mixer', True):
        keep = ('xT', 'cT', 'w_ada', 'b_ada', 'gains', 'f1i', 'f1o', 'f2i', 'f2o')
    res = run_bass_kernel_spmd(nc, in_maps, core_ids=list(range(NCORES)))
    return res.results


def assemble(results):
    B, Bs = 16, 32
    f = np.float32
    yp = np.zeros((B, TP, D), f)
    ys = np.zeros((Bs, TS, D), f)
    conv_p = np.zeros((1, B, 3, 3072), f)
    conv_s = np.zeros((1, Bs, 3, 3072), f)
    S_p = np.zeros((1, B, 8, 128, 128), f)
    S_s = np.zeros((1, Bs, 8, 128, 128), f)
    k_p = np.zeros((1, B, 128, 4, 64), f)
    v_p = np.zeros((1, B, 128, 4, 64), f)
    k_s = np.zeros((1, Bs, 128, 4, 64), f)
    v_s = np.zeros((1, Bs, 128, 4, 64), f)
    for c in range(NCORES):
        r = results[c]
        y = r['yT'].T
        yp[2 * c:2 * c + 2] = y[:2 * TP].reshape(2, TP, D)
        ys[4 * c:4 * c + 4] = y[2 * TP:].reshape(4, TS, D)
        h = r['histout']
        hh = h.transpose(2, 3, 1, 0).reshape(NSEQ, 3, 3072)
        conv_p[0, 2 * c:2 * c + 2] = hh[0:2]
        conv_s[0, 4 * c:4 * c + 4] = hh[2:6]
        S = r['Sout'].transpose(1, 2, 0, 3)
        S_p[0, 2 * c:2 * c + 2] = S[0:2]
        S_s[0, 4 * c:4 * c + 4] = S[2:6]
        ko = r['kout'][0:64].transpose(1, 3, 2, 0)
        vo = r['vout'][0:64].transpose(1, 3, 2, 0)
        k_p[0, 2 * c:2 * c + 2] = ko[0:2]
        k_s[0, 4 * c:4 * c + 4] = ko[2:6]
        v_p[0, 2 * c:2 * c + 2] = vo[0:2]
        v_s[0, 4 * c:4 * c + 4] = vo[2:6]
    return (yp, ys, conv_p, S_p, k_p, v_p, conv_s, S_s, k_s, v_s)


def kernel(**inputs):
    cfg = {'tiles': make_tiles()}
    results = run(inputs, cfg)
    return assemble(results)
```
